# Optimizing a Trainium2 kernel written in Bass

```python
import math
import jax
import jax.numpy as jnp
from jax import lax
import numpy as np

D_MODEL = 1024
BATCH = 8
SEQ = 2048
DEPTH = 4

EPS = 1e-6
N_EVEN = (DEPTH + 1) // 2
N_ODD = DEPTH // 2

SB_HEAD_DIM = 64
SB_WIDTH = D_MODEL // 2
SB_HEADS = SB_WIDTH // SB_HEAD_DIM
SB_BLOCK = 128
SGU_GROUP_DIM = 64
SGU_WIDTH = D_MODEL // 2
SGU_GROUPS = SGU_WIDTH // SGU_GROUP_DIM
SGU_CHUNK = 128
AB_IN = 3 * SB_WIDTH + 2 * SGU_WIDTH
AB_MIX = SB_WIDTH + SGU_WIDTH
SSD_EXPAND = 2
D_INNER = SSD_EXPAND * D_MODEL
SSD_HEAD_DIM = 64
SSD_HEADS = D_INNER // SSD_HEAD_DIM
SSD_GROUPS = 8
SSD_HEADS_PER_GROUP = SSD_HEADS // SSD_GROUPS
SSD_STATE = 128
SSD_CONV = 4
SSD_CHUNK = 128
SSD_CONV_DIM = D_INNER + 2 * SSD_GROUPS * SSD_STATE
SSD_IN = D_INNER + SSD_CONV_DIM + SSD_HEADS
D_FF = ((8 * D_MODEL // 3 + 127) // 128) * 128
FFN_CONV = 3

kernel_name = "hybrid_stickbreak_sgu_ssd_convffn"


def rms_norm(x, g):
    xf = x.astype(jnp.float32)
    y = xf * lax.rsqrt(jnp.mean(xf * xf, axis=-1, keepdims=True) + EPS)
    return (y * g.astype(jnp.float32)).astype(x.dtype)


def causal_dw_conv(x, w, b):
    k_w, ch = w.shape
    y = lax.conv_general_dilated(
        x, w[:, None, :].astype(x.dtype), window_strides=(1,),
        padding=[(k_w - 1, 0)], dimension_numbers=('NWC', 'WIO', 'NWC'),
        feature_group_count=ch)
    return y + b.astype(x.dtype)


def stick_breaking_attention(q, k, v):
    seq = q.shape[1]
    scale = q.shape[-1] ** -0.5
    outs = []
    for blk in range(seq // SB_BLOCK):
        s0 = blk * SB_BLOCK
        e = s0 + SB_BLOCK
        z = jnp.einsum('bqhd,bkhd->bhqk', q[:, s0:e], k[:, :e]).astype(jnp.float32) * scale
        t_pos = s0 + jnp.arange(SB_BLOCK)[:, None]
        s_pos = jnp.arange(e)[None, :]
        strict = s_pos < t_pos
        log_keep = jnp.where(strict, jax.nn.log_sigmoid(-z), 0.0)
        between = lax.cumsum(log_keep, axis=3, reverse=True) - log_keep
        w = jnp.where(strict, jnp.exp(jax.nn.log_sigmoid(z) + between), 0.0)
        outs.append(jnp.einsum('bhqk,bkhd->bqhd', w.astype(v.dtype), v[:, :e]))
    return jnp.concatenate(outs, axis=1)


def spatial_gating(u, v, w_s, b_s):
    bsz, seq, _ = u.shape
    nc = seq // SGU_CHUNK
    shp = (bsz, nc, SGU_CHUNK, SGU_GROUPS, SGU_GROUP_DIM)
    u = u.reshape(shp)
    vf = v.astype(jnp.float32).reshape(shp)
    mu = jnp.mean(vf, axis=-1, keepdims=True)
    var = jnp.mean(jnp.square(vf - mu), axis=-1, keepdims=True)
    vn = ((vf - mu) * lax.rsqrt(var + EPS)).astype(u.dtype)
    causal = jnp.tril(jnp.ones((SGU_CHUNK, SGU_CHUNK), dtype=bool))
    w = jnp.where(causal, w_s, jnp.zeros_like(w_s))
    mixed = jnp.einsum('gts,bcsgd->bctgd', w, vn) + b_s.T[:, :, None]
    return (u * mixed).reshape(bsz, seq, SGU_WIDTH)


def attn_sgu_mixer(h, w_in, sgu_w, sgu_b, w_out):
    bsz, seq, _ = h.shape
    proj = h @ w_in
    q, k, v, u_g, v_g = jnp.split(
        proj, [SB_WIDTH, 2 * SB_WIDTH, 3 * SB_WIDTH, 3 * SB_WIDTH + SGU_WIDTH], axis=-1)
    heads = lambda t: t.reshape(bsz, seq, SB_HEADS, SB_HEAD_DIM)
    o_a = stick_breaking_attention(heads(q), heads(k), heads(v)).reshape(bsz, seq, SB_WIDTH)
    o_b = spatial_gating(jax.nn.gelu(u_g), jax.nn.gelu(v_g), sgu_w, sgu_b)
    return jnp.concatenate([o_a, o_b], axis=-1) @ w_out


def ssd_scan(x, a, bm, cm):
    bsz, seq = x.shape[:2]
    nc = seq // SSD_CHUNK
    x = x.reshape(bsz, nc, SSD_CHUNK, SSD_GROUPS, SSD_HEADS_PER_GROUP, SSD_HEAD_DIM)
    a = a.reshape(bsz, nc, SSD_CHUNK, SSD_GROUPS, SSD_HEADS_PER_GROUP)
    bm = bm.reshape(bsz, nc, SSD_CHUNK, SSD_GROUPS, SSD_STATE)
    cm = cm.reshape(bsz, nc, SSD_CHUNK, SSD_GROUPS, SSD_STATE)
    a_cum = jnp.cumsum(a, axis=2)
    a_t = jnp.moveaxis(a_cum, 2, -1)
    seg = a_t[..., :, None] - a_t[..., None, :]
    causal = jnp.tril(jnp.ones((SSD_CHUNK, SSD_CHUNK), dtype=bool))
    l_mat = jnp.exp(jnp.where(causal, seg, -jnp.inf))
    cb = jnp.einsum('bclgn,bcsgn->bcgls', cm, bm)
    y_diag = jnp.einsum('bcgrls,bcsgrp->bclgrp', cb[:, :, :, None] * l_mat, x)
    decay = jnp.exp(a_cum[:, :, -1:] - a_cum)
    states = jnp.einsum('bclgn,bclgrp->bcgrpn', bm, x * decay[..., None])
    chunk_decay = jnp.exp(a_cum[:, :, -1])

    def step(hs, inp):
        st, dec = inp
        return hs * dec[..., None, None] + st, hs

    h0 = jnp.zeros((bsz, SSD_GROUPS, SSD_HEADS_PER_GROUP, SSD_HEAD_DIM, SSD_STATE), jnp.float32)
    _, h_prev = lax.scan(step, h0, (jnp.moveaxis(states, 1, 0), jnp.moveaxis(chunk_decay, 1, 0)))
    h_prev = jnp.moveaxis(h_prev, 0, 1)
    y_off = jnp.einsum('bclgn,bcgrpn->bclgrp', cm, h_prev) * jnp.exp(a_cum)[..., None]
    return (y_diag + y_off).reshape(bsz, seq, SSD_HEADS, SSD_HEAD_DIM)


def mamba2_mixer(h, w_in, conv_w, conv_b, dt_bias, a_log, d_skip, norm_g, w_out):
    bsz, seq, _ = h.shape
    proj = h @ w_in
    z, xbc, dt = jnp.split(proj, [D_INNER, D_INNER + SSD_CONV_DIM], axis=-1)
    xbc = jax.nn.silu(causal_dw_conv(xbc, conv_w, conv_b))
    xs, bm, cm = jnp.split(xbc, [D_INNER, D_INNER + SSD_GROUPS * SSD_STATE], axis=-1)
    dt = jax.nn.softplus(dt.astype(jnp.float32) + dt_bias.astype(jnp.float32))
    a_neg = -jnp.exp(a_log.astype(jnp.float32))
    xh = xs.astype(jnp.float32).reshape(bsz, seq, SSD_HEADS, SSD_HEAD_DIM)
    y = ssd_scan(xh * dt[..., None], a_neg * dt,
                 bm.astype(jnp.float32).reshape(bsz, seq, SSD_GROUPS, SSD_STATE),
                 cm.astype(jnp.float32).reshape(bsz, seq, SSD_GROUPS, SSD_STATE))
    y = y + xh * d_skip.astype(jnp.float32)[:, None]
    yg = (y.reshape(bsz, seq, D_INNER) * jax.nn.silu(z.astype(jnp.float32)))
    yg = yg.reshape(bsz, seq, SSD_GROUPS, D_INNER // SSD_GROUPS)
    yg = yg * lax.rsqrt(jnp.mean(yg * yg, axis=-1, keepdims=True) + EPS)
    yg = yg.reshape(bsz, seq, D_INNER) * norm_g.astype(jnp.float32)
    return yg.astype(h.dtype) @ w_out


def conv_ffn(h, w_up, conv_w, conv_b, w_down):
    hu = causal_dw_conv(h @ w_up, conv_w, conv_b)
    g, v = jnp.split(hu, 2, axis=-1)
    return (jax.nn.gelu(g) * v) @ w_down


def setup_inputs(seed: int = 0) -> dict:
    key = jax.random.key(seed)
    ks = jax.random.split(key, 24)
    f32 = jnp.float32
    nrm = lambda k, shape, s: jax.random.normal(k, shape, f32) * s
    gain = lambda k, shape: 1.0 + 0.05 * jax.random.normal(k, shape, f32)
    dt0 = jnp.exp(jax.random.uniform(ks[13], (N_ODD, SSD_HEADS), f32)
                  * (math.log(0.1) - math.log(0.001)) + math.log(0.001))
    return {
        "x": nrm(ks[0], (BATCH, SEQ, D_MODEL), 1.0),
        "mix_pre_g": gain(ks[1], (DEPTH, D_MODEL)),
        "mix_post_g": gain(ks[2], (DEPTH, D_MODEL)),
        "ffn_pre_g": gain(ks[3], (DEPTH, D_MODEL)),
        "ffn_post_g": gain(ks[4], (DEPTH, D_MODEL)),
        "ab_w_in": nrm(ks[5], (N_EVEN, D_MODEL, AB_IN), D_MODEL ** -0.5),
        "sgu_w": nrm(ks[6], (N_EVEN, SGU_GROUPS, SGU_CHUNK, SGU_CHUNK), SGU_CHUNK ** -0.5),
        "sgu_b": 1.0 + nrm(ks[7], (N_EVEN, SGU_GROUPS, SGU_CHUNK), 0.02),
        "ab_w_out": nrm(ks[8], (N_EVEN, AB_MIX, D_MODEL), AB_MIX ** -0.5),
        "ssd_w_in": nrm(ks[9], (N_ODD, D_MODEL, SSD_IN), D_MODEL ** -0.5),
        "ssd_conv_w": nrm(ks[10], (N_ODD, SSD_CONV, SSD_CONV_DIM), SSD_CONV ** -0.5),
        "ssd_conv_b": nrm(ks[11], (N_ODD, SSD_CONV_DIM), 0.02),
        "ssd_dt_bias": dt0 + jnp.log(-jnp.expm1(-dt0)),
        "ssd_a_log": jnp.log(jax.random.uniform(ks[14], (N_ODD, SSD_HEADS), f32, 1.0, 16.0)),
        "ssd_d": gain(ks[15], (N_ODD, SSD_HEADS)),
        "ssd_norm_g": gain(ks[16], (N_ODD, D_INNER)),
        "ssd_w_out": nrm(ks[17], (N_ODD, D_INNER, D_MODEL), D_INNER ** -0.5),
        "ffn_w_up": nrm(ks[18], (DEPTH, D_MODEL, 2 * D_FF), D_MODEL ** -0.5),
        "ffn_conv_w": nrm(ks[19], (DEPTH, FFN_CONV, 2 * D_FF), FFN_CONV ** -0.5),
        "ffn_conv_b": nrm(ks[20], (DEPTH, 2 * D_FF), 0.02),
        "ffn_w_down": nrm(ks[21], (DEPTH, D_FF, D_MODEL), D_FF ** -0.5),
    }


def reference(x, mix_pre_g, mix_post_g, ffn_pre_g, ffn_post_g,
              ab_w_in, sgu_w, sgu_b, ab_w_out,
              ssd_w_in, ssd_conv_w, ssd_conv_b, ssd_dt_bias, ssd_a_log, ssd_d,
              ssd_norm_g, ssd_w_out,
              ffn_w_up, ffn_conv_w, ffn_conv_b, ffn_w_down):
    for i in range(DEPTH):
        j = i // 2
        h = rms_norm(x, mix_pre_g[i])
        if i % 2 == 0:
            m = attn_sgu_mixer(h, ab_w_in[j], sgu_w[j], sgu_b[j], ab_w_out[j])
        else:
            m = mamba2_mixer(h, ssd_w_in[j], ssd_conv_w[j], ssd_conv_b[j], ssd_dt_bias[j],
                             ssd_a_log[j], ssd_d[j], ssd_norm_g[j], ssd_w_out[j])
        x = x + rms_norm(m, mix_post_g[i])
        h = rms_norm(x, ffn_pre_g[i])
        f = conv_ffn(h, ffn_w_up[i], ffn_conv_w[i], ffn_conv_b[i], ffn_w_down[i])
        x = x + rms_norm(f, ffn_post_g[i])
    return x
```

```python
import numpy as np
from contextlib import ExitStack
import concourse.bass as bass
import concourse.mybir as mybir
from concourse.bass_utils import run_bass_kernel_spmd

F32 = mybir.dt.float32
BF16 = mybir.dt.bfloat16
AF = mybir.ActivationFunctionType
ALU = mybir.AluOpType
AX = mybir.AxisListType

D = 1024
SEQ = 2048
NB = 8
DEPTH = 4
T = 512
NT = SEQ // T
DFF = 2816
NPAIR = DFF // 128
AB_IN = 2560
SSD_IN = 6176
EPS = 1e-6

def _build_pcols():
    cols = {}
    n = 0
    def add(name, w):
        nonlocal n
        cols[name] = (n, w)
        n += w
    for l in range(DEPTH):
        for k in ("mix_pre", "mix_post", "ffn_pre", "ffn_post"):
            add(f"{k}{l}", 8)
        for k in range(3):
            add(f"fcw{l}_{k}", 44)
        add(f"fcb{l}", 44)
    for j in range(2):
        for k in range(4):
            add(f"scw{j}_{k}", 32)
        add(f"scb{j}", 32)
        add(f"sng{j}", 16)
        add(f"sd{j}", 16)
    return cols, n

PCOLS, NPV = _build_pcols()

RCOLS = {}
_n = 0
for j in range(2):
    RCOLS[f"dtb{j}"] = (_n, 32); _n += 32
    RCOLS[f"alog{j}"] = (_n, 32); _n += 32
NRV = _n

C_ID, C_ONE, C_GE, C_LE, C_GT, C_AM = 0, 128, 256, 384, 512, 640
NCST = 640 + 896


def _consts():
    a = np.arange(128)[:, None]
    b = np.arange(128)[None, :]
    c = np.zeros((128, NCST), np.float32)
    c[:, C_ID:C_ID + 128] = (a == b)
    c[:, C_ONE:C_ONE + 128] = 1.0
    c[:, C_GE:C_GE + 128] = (a >= b)
    c[:, C_LE:C_LE + 128] = (a <= b)
    c[:, C_GT:C_GT + 128] = (a > b)
    u = np.arange(896)[None, :]
    c[:, C_AM:C_AM + 896] = (a < (u - 384))
    return c


class Sync:
    NS = 8
    SAME_GAP = 6

    def __init__(self, nc, es):
        self.nc = nc
        self.engs = {"pe": nc.tensor, "act": nc.scalar, "dve": nc.vector, "pool": nc.gpsimd, "sp": nc.sync}
        self.semh = {}
        for e in self.engs:
            self.semh[e] = es.enter_context(nc.semaphore(f"s_{e}"))
        for q in ("sp", "pool"):
            for i in range(self.NS):
                self.semh[("dma", q, i)] = es.enter_context(nc.semaphore(f"d_{q}{i}"))
        self.cnt = {e: 0 for e in self.engs}
        self.known = {e: {} for e in self.engs}
        self.lastw = {}
        self.readers = {}
        self.dma_k = {"sp": 0, "pool": 0}
        self.n_wait = 0
        self.n_ins = 0
        self._cur_wide = False
        self.wide_tk = set()

    def need(self, e, tk):
        if tk is None:
            return
        key, val = tk
        if key == e and e != "pool" and (e == "pe" or self.cnt[e] - val >= self.SAME_GAP):
            return
        if key == e and e != "pool" and self._cur_wide and (e, val) in self.wide_tk:
            return
        if self.known[e].get(key, 0) >= val:
            return
        self.engs[e].wait_ge(self.semh[key], val)
        self.known[e][key] = val
        self.n_wait += 1

    def _deps(self, e, r, w):
        for res in r:
            self.need(e, self.lastw.get(res))
        for res in w:
            self.need(e, self.lastw.get(res))
            for k, v in self.readers.get(res, {}).items():
                if k != e or e == "pool":
                    self.need(e, (k, v))

    def _record(self, tk, r, w):
        key, val = tk
        for res in r:
            self.readers.setdefault(res, {})[key] = val
        for res in w:
            self.lastw[res] = tk
            self.readers[res] = {}

    WIDE = ("scr", "m_sb", "hT", "x0", "x1", "x2", "x3", "actT", "sp", "Rb", "w_t", "mbs")

    def op(self, e, fn, r=(), w=()):
        self._cur_wide = len(w) > 0 and all(res.startswith(self.WIDE) for res in w)
        self._deps(e, r, w)
        ins = fn(self.engs[e])
        self.cnt[e] += 1
        ins.then_inc(self.semh[e], 1)
        if self._cur_wide:
            self.wide_tk.add((e, self.cnt[e]))
        self._cur_wide = False
        self._record((e, self.cnt[e]), r, w)
        self.n_ins += 1

    def dma(self, q, out, in_, r=(), w=()):
        k = self.dma_k[q]
        self.dma_k[q] = k + 1
        key = ("dma", q, k % self.NS)
        val = 16 * (k // self.NS + 1)
        if k >= self.NS:
            self.need(q, (key, val - 16))
        self._deps(q, r, w)
        ins = self.engs[q].dma_start(out=out, in_=in_)
        ins.then_inc(self.semh[key], 16)
        tk = (key, val)
        self._record(tk, r, w)
        self.n_ins += 1
        return tk


class Builder:
    def __init__(self, n_layers=DEPTH, do_mixer=True, do_ffn=True):
        self.n_layers = n_layers
        self.do_mixer = do_mixer
        self.do_ffn = do_ffn

    def pcol(self, name, i=0, w=1):
        o, _ = PCOLS[name]
        return self.pv[:, o + i:o + i + w]

    def rrow(self, name, a=0, b=None):
        o, wd = RCOLS[name]
        if b is None:
            b = wd
        return self.rv[:, o + a:o + b]

    def cb(self, off, w=128):
        return self.cstb[:, off:off + w]

    def amask(self, kl):
        o = C_AM + (3 - kl) * 128
        return self.cstb[:, o:o + 512]

    def scr(self, i, a=0, b=T):
        return self.SCR[:, i, a:b]

    def ftile(self, k):
        if k < 6:
            return self.SCR[:, k, :], f"scr{k}"
        o = (k - 6) * (T + 4)
        assert o + T + 4 <= 4096
        return self.MBf[:, o:o + T + 4], f"mbs{k - 6}"

    def scr_bf(self, i0, c):
        i = i0 + c // 2
        v = self.SCR[:, i, 0:512].bitcast(BF16)
        return v[:, (c % 2) * 512:(c % 2 + 1) * 512]

    def build(self):
        nc = bass.Bass("TRN2", target_bir_lowering=False)
        self.nc = nc
        dr = {}
        def din(name, shape):
            dr[name] = nc.dram_tensor(name, list(shape), F32, kind="ExternalInput").ap()
        din("xT", (128, 8 * SEQ))
        din("pvec", (128, NPV))
        din("rvec", (128, NRV))
        din("cst", (128, NCST))
        for j in range(2):
            din(f"ab_w_in{j}", (128, 8, AB_IN))
            din(f"ab_w_out{j}", (128, 8, D))
            din(f"sgu_wT{j}", (128, 8, 128))
            din(f"sgub{j}", (128, 2048))
            din(f"ssd_w_in{j}", (128, 8, SSD_IN))
            din(f"ssd_w_out{j}", (128, 16, D))
        for l in range(DEPTH):
            din(f"ffn_w_up{l}", (128, NPAIR, 8, 256))
            din(f"ffn_w_down{l}", (128, 8, NPAIR, 128))
        dr["yT"] = nc.dram_tensor("yT", [128, 8 * SEQ], F32, kind="ExternalOutput").ap()
        self.dr = dr

        with ExitStack() as es:
            self.es = es
            S = Sync(nc, es)
            self.S = S
            def sb(name, shape, dt):
                return es.enter_context(nc.sbuf_tensor(name, list(shape), dt))
            self.xT = sb("xT_sb", (128, 8, SEQ), F32)
            self.pv = sb("pv", (128, NPV), F32)
            self.rv = sb("rv", (128, NRV), F32)
            self.cstb = sb("cstb", (128, NCST), BF16)
            self.kc = sb("kc", (128, 2), F32)
            self.NWB = 3
            self.wb = [sb(f"wb{i}", (128, 4096), BF16) for i in range(self.NWB)]
            self.ps = [es.enter_context(nc.psum_tensor(f"ps{i}", [128, 512], F32)) for i in range(8)]
            self.sq = sb("sq", (128, 2, T), BF16)
            self.rstd = sb("rstd", (128, T), F32)
            self.hT = sb("hT", (128, 8, T), BF16)
            self.MB = sb("MB", (128, 16, T), BF16)
            self.m_sb = self.MB[:].bitcast(F32).rearrange("p a b -> p (a b)").rearrange("p (c t) -> p c t", c=8)
            self.SCR = sb("SCR", (128, 6, T + 4), F32)
            self.MBf = self.MB[:].bitcast(F32).rearrange("p a b -> p (a b)")
            self.U24 = sb("U24", (128, 24, T), BF16)

            S.dma("sp", self.pv[:], dr["pvec"][:, :], w=["pv"])
            S.dma("sp", self.rv[:], dr["rvec"][:, :], w=["rv"])
            S.dma("pool", self.cstb[:], dr["cst"][:, :], w=["cst"])
            S.op("dve", lambda e: e.memset(self.kc[:, 0:1], EPS), w=["kc"])
            S.op("dve", lambda e: e.memset(self.kc[:, 1:2], 1.0), w=["kc"])
            for c in range(8):
                S.dma("sp", self.xT[:, c, :], dr["xT"][:, c * SEQ:(c + 1) * SEQ], w=[f"x{tt}" for tt in range(NT)])

            self.items = []
            for l in (getattr(self, "layer_list", None) or range(self.n_layers)):
                j = l // 2
                if l % 2 == 0:
                    self.even_layer(l, j)
                else:
                    self.odd_layer(l, j)
            if getattr(self, "max_items", None):
                self.items = self.items[:self.max_items]
            self.run_items()
            dbg_tks = []
            if getattr(self, "dbg", None):
                self.barrier()
                for name, apf, shape, dt in self.dbg:
                    o = nc.dram_tensor(name, list(shape), dt, kind="ExternalOutput").ap()
                    dbg_tks.append(S.dma("sp", o, apf(self)))
            for tk in dbg_tks:
                S.need("sp", tk)

            tks = []
            for c in range(8):
                tks.append(S.dma("sp", dr["yT"][:, c * SEQ:(c + 1) * SEQ], self.xT[:, c, :],
                                 r=[f"x{tt}" for tt in range(NT)]))
            for tk in tks:
                S.need("sp", tk)
            if getattr(self, "_es2", None):
                self._es2.close()
            print(f"[build] instructions={S.n_ins} waits={S.n_wait}")
        return nc

    def item(self, fn, load=None):
        self.items.append((fn, load))

    def run_items(self):
        S = self.S
        loads = [i for i, (fn, ld) in enumerate(self.items) if ld is not None]
        views = {}
        ptr = 0
        consumed = 0
        for idx, (fn, ld) in enumerate(self.items):
            while ptr < len(loads) and ptr - consumed < self.NWB:
                li = loads[ptr]
                src, shape = self.items[li][1]
                bi = ptr % self.NWB
                buf = self.wb[bi]
                n = int(np.prod(shape))
                assert n <= 4096, shape
                v = buf[:, 0:n]
                if len(shape) == 2:
                    v = v.rearrange("p (a b) -> p a b", a=shape[0])
                elif len(shape) == 3:
                    v = v.rearrange("p (a b c) -> p a b c", a=shape[0], b=shape[1])
                S.dma("pool", v, src, w=[f"wb{bi}"])
                views[li] = (v, f"wb{bi}")
                ptr += 1
            if ld is not None:
                v, res = views.pop(idx)
                fn(v, res)
                consumed += 1
            else:
                fn(None, None)

    def sumsq_rstd(self, srcs, src_res, nfeat, width=T, ps_i=6):
        S = self.S
        n = len(srcs)
        ps = self.ps[ps_i]
        for c, s in enumerate(srcs):
            b = c % 2
            S.op("act", lambda e, b=b, s=s: e.activation(out=self.sq[:, b, 0:width], in_=s, func=AF.Square),
                 r=src_res, w=[f"sq{b}"])
            S.op("pe", lambda e, c=c, b=b: e.matmul(ps[:, 0:width], lhsT=self.cb(C_ONE), rhs=self.sq[:, b, 0:width],
                                                    start=(c == 0), stop=(c == n - 1)),
                 r=[f"sq{b}", "cst"], w=[f"ps{ps_i}"])
        S.op("act", lambda e: e.activation(out=self.rstd[:, 0:width], in_=ps[:, 0:width], func=AF.Ln,
                                           bias=self.kc[:, 0:1], scale=1.0 / nfeat),
             r=[f"ps{ps_i}", "kc"], w=["rstd"])
        S.op("act", lambda e: e.activation(out=self.rstd[:, 0:width], in_=self.rstd[:, 0:width], func=AF.Exp, scale=-0.5),
             r=["rstd"], w=["rstd"])

    def pre_norm(self, gname, tt):
        S = self.S
        ts = slice(tt * T, (tt + 1) * T)
        self.sumsq_rstd([self.xT[:, c, ts] for c in range(8)], [f"x{tt}"], D)
        for c in range(8):
            S.op("dve", lambda e, c=c: e.scalar_tensor_tensor(
                out=self.hT[:, c, :], in0=self.xT[:, c, ts], scalar=self.pcol(gname, c), in1=self.rstd[:],
                op0=ALU.mult, op1=ALU.mult), r=[f"x{tt}", "rstd", "pv"], w=["hT"])

    def post_norm_residual(self, gname, tt):
        S = self.S
        ts = slice(tt * T, (tt + 1) * T)
        self.sumsq_rstd([self.m_sb[:, c, :] for c in range(8)], ["m_sb"], D)
        for c in range(8):
            S.op("dve", lambda e, c=c: e.scalar_tensor_tensor(
                out=self.m_sb[:, c, :], in0=self.m_sb[:, c, :], scalar=self.pcol(gname, c), in1=self.rstd[:],
                op0=ALU.mult, op1=ALU.mult), r=["m_sb", "rstd", "pv"], w=["m_sb"])
            S.op("dve", lambda e, c=c: e.tensor_tensor(out=self.xT[:, c, ts], in0=self.xT[:, c, ts],
                                                       in1=self.m_sb[:, c, :], op=ALU.add),
                 r=["m_sb", f"x{tt}"], w=[f"x{tt}"])

    def out_proj(self, w_ap_fn, nk, src_fn, src_res):
        S = self.S
        for n0 in range(0, 8, 2):
            def fn(wv, wres, n0=n0):
                for dn in range(2):
                    n = n0 + dn
                    pi = n % 2
                    ps = self.ps[pi]
                    for k in range(nk):
                        S.op("pe", lambda e, k=k, dn=dn, ps=ps: e.matmul(
                            ps[:], lhsT=wv[:, k, dn * 128:(dn + 1) * 128], rhs=src_fn(k),
                            start=(k == 0), stop=(k == nk - 1)), r=[wres] + src_res, w=[f"ps{pi}"])
                    S.op("act", lambda e, n=n, ps=ps: e.activation(out=self.m_sb[:, n, :], in_=ps[:], func=AF.Copy),
                         r=[f"ps{pi}"], w=["m_sb"])
            self.item(fn, load=(w_ap_fn(n0), (nk, 256)))

    def barrier(self):
        S = self.S
        for e in ("pe", "act", "dve", "pool", "sp"):
            for o in ("pe", "act", "dve", "pool"):
                if o != e and S.cnt[o] > 0:
                    S.need(e, (o, S.cnt[o]))
            for q in ("sp", "pool"):
                k = S.dma_k[q]
                for sl in range(min(k, S.NS)):
                    last = k - 1 - ((k - 1 - sl) % S.NS)
                    S.need(e, (("dma", q, sl), 16 * (last // S.NS + 1)))

    def ffn(self, l, tt, mid=None):
        S = self.S
        dr = self.dr
        actT = self.U24
        self.item(lambda wv, wr: self.pre_norm(f"ffn_pre{l}", tt))
        for i0 in range(0, NPAIR, 2):
            def fn(wv, wres, i0=i0):
                pending = []
                for di in range(2):
                    i = i0 + di
                    st_ = (i % 3) * 4
                    for half in range(2):
                        ch = i + half * NPAIR
                        pi = 2 + half + 2 * (i % 2)
                        ps = self.ps[pi]
                        for k in range(8):
                            S.op("pe", lambda e, k=k, di=di, half=half, ps=ps: e.matmul(
                                ps[:], lhsT=wv[:, di, k, half * 128:(half + 1) * 128], rhs=self.hT[:, k, :],
                                start=(k == 0), stop=(k == 7)), r=[wres, "hT"], w=[f"ps{pi}"])
                        hu, hres = self.ftile(st_ + half)
                        acc, ares = self.ftile(st_ + 2 + half)
                        S.op("pool", lambda e, ch=ch, hu=hu: e.tensor_copy(out=hu[:, 0:2], in_=self.fhalo[:, ch, :]),
                             r=[f"fhalo{ch}"], w=[hres + "h"])
                        S.op("act", lambda e, hu=hu, ps=ps: e.activation(out=hu[:, 2:2 + T], in_=ps[:], func=AF.Copy),
                             r=[f"ps{pi}"], w=[hres])
                        S.op("pool", lambda e, ch=ch, hu=hu: e.tensor_copy(out=self.fhalo[:, ch, :], in_=hu[:, T:T + 2]),
                             r=[hres], w=[f"fhalo{ch}"])
                        S.op("act", lambda e, ch=ch, hu=hu, acc=acc: e.activation(
                            out=acc[:, 0:T], in_=hu[:, 0:T], func=AF.Identity,
                            bias=self.pcol(f"fcb{l}", ch), scale=self.pcol(f"fcw{l}_0", ch)),
                            r=[hres, hres + "h", "pv"], w=[ares])
                    for f in pending:
                        f()
                    pending = []
                    for half in range(2):
                        ch = i + half * NPAIR
                        hu, hres = self.ftile(st_ + half)
                        acc, ares = self.ftile(st_ + 2 + half)
                        for kk in (1, 2):
                            S.op("dve", lambda e, ch=ch, hu=hu, acc=acc, kk=kk: e.scalar_tensor_tensor(
                                out=acc[:, 0:T], in0=hu[:, kk:kk + T], scalar=self.pcol(f"fcw{l}_{kk}", ch),
                                in1=acc[:, 0:T], op0=ALU.mult, op1=ALU.add), r=[hres, hres + "h", "pv", ares], w=[ares])
                    ag, agres = self.ftile(st_ + 2)
                    av, avres = self.ftile(st_ + 3)
                    pending.append(lambda ag=ag, agres=agres: S.op(
                        "act", lambda e: e.activation(out=ag[:, 0:T], in_=ag[:, 0:T], func=AF.Gelu_apprx_tanh), r=[agres], w=[agres]))
                    pending.append(lambda i=i, ag=ag, av=av, agres=agres, avres=avres: S.op(
                        "pool", lambda e: e.tensor_tensor(out=actT[:, i, :], in0=ag[:, 0:T], in1=av[:, 0:T], op=ALU.mult),
                        r=[agres, avres], w=["actT"]))
                for f in pending:
                    f()
            self.item(fn, load=(dr[f"ffn_w_up{l}"][:, i0:i0 + 2, :, :], (2, 8, 256)))
        for n in range(8):
            def fn(wv, wres, n=n):
                pi = n % 2
                ps = self.ps[pi]
                for k in range(NPAIR):
                    S.op("pe", lambda e, k=k, ps=ps: e.matmul(ps[:], lhsT=wv[:, k, :], rhs=actT[:, k, :],
                                                              start=(k == 0), stop=(k == NPAIR - 1)),
                         r=[wres, "actT"], w=[f"ps{pi}"])
                S.op("act", lambda e, ps=ps: e.activation(out=self.m_sb[:, n, :], in_=ps[:], func=AF.Copy),
                     r=[f"ps{pi}"], w=["m_sb"])
            self.item(fn, load=(dr[f"ffn_w_down{l}"][:, n, :, :], (NPAIR, 128)))
            if n == 3 and mid is not None:
                mid()
        self.item(lambda wv, wr: self.post_norm_residual(f"ffn_post{l}", tt))

    def even_layer(self, l, j):
        S = self.S
        dr = self.dr
        nc = self.nc
        es2 = ExitStack()
        def sb(name, shape, dt):
            return es2.enter_context(nc.sbuf_tensor(f"{name}_{l}", list(shape), dt))

        def begin(wv, wr):
            self.barrier()
            self._es2 = es2
            self.nkT = sb("nkT", (128, 4, SEQ), BF16)
            self.v_sb = sb("v_sb", (128, 16, 512), BF16)
            self.wsT = sb("wsT", (128, 8, 128), BF16)
            self.qT = sb("qT", (128, 4, T), BF16)
            self.Rb = [sb(f"Rb{i}", (128, T), BF16) for i in range(2)]
            self.w_t = [sb(f"w_t{i}", (128, T), BF16) for i in range(2)]
            self.st8 = sb("st8", (128, 8), F32)
            self.st8b = sb("st8b", (128, 8), F32)
            self.fhalo = sb("fhalo", (128, 44, 2), F32)
            print("[sbuf] even layer remaining bytes/partition:", nc.sbuf_bytes_remaining)
            S.op("dve", lambda e: e.memset(self.fhalo[:], 0.0), w=[f"fhalo{c_}" for c_ in range(44)])
            for i_ in range(2):
                S.op("dve", lambda e, i_=i_: e.memset(self.w_t[i_][:], 0.0), w=[f"w_t{i_}"])
            S.dma("pool", self.wsT[:], dr[f"sgu_wT{j}"][:, :, :], w=["wsT"])
            S.op("dve", lambda e: e.tensor_tensor(
                out=self.wsT[:], in0=self.wsT[:],
                in1=self.cb(C_LE).unsqueeze(1).to_broadcast([128, 8, 128]), op=ALU.mult),
                r=["wsT", "cst"], w=["wsT"])
        self.item(begin)
        for tt in range(NT):
            hoist = self.do_mixer and self.do_ffn and tt + 1 < NT
            if self.do_mixer:
                self.even_mixer_tile(l, j, tt, skip_pre=(hoist_prev if tt > 0 else False))
            if self.do_ffn:
                self.ffn(l, tt, mid=(lambda tt=tt: self.item(lambda wv, wr: self.pre_norm(f"mix_pre{l}", tt + 1))) if hoist else None)
            hoist_prev = hoist
        def end(wv, wr):
            self.barrier()
            es2.close()
            self._es2 = None
        self.item(end)

    def even_mixer_tile(self, l, j, tt, skip_pre=False):
        S = self.S
        dr = self.dr
        ts = slice(tt * T, (tt + 1) * T)
        mixT = self.hT
        sp_all = self.U24
        if not skip_pre:
            self.item(lambda wv, wr: self.pre_norm(f"mix_pre{l}", tt))
        W = dr[f"ab_w_in{j}"]

        def proj_fm(wv, wres, kind):
            for c in range(4):
                pi = c % 4
                ps = self.ps[pi]
                for k in range(8):
                    S.op("pe", lambda e, k=k, c=c, ps=ps: e.matmul(ps[:], lhsT=wv[:, k, c * 128:(c + 1) * 128],
                                                                   rhs=self.hT[:, k, :], start=(k == 0), stop=(k == 7)),
                         r=[wres, "hT"], w=[f"ps{pi}"])
                if kind == "q":
                    S.op("act", lambda e, c=c, ps=ps: e.activation(out=self.qT[:, c, :], in_=ps[:], func=AF.Copy, scale=0.125),
                         r=[f"ps{pi}"], w=["qT"])
                elif kind == "k":
                    S.op("act", lambda e, c=c, ps=ps: e.activation(out=self.nkT[:, c, ts], in_=ps[:], func=AF.Copy, scale=-1.0),
                         r=[f"ps{pi}"], w=["nkT"])
                else:
                    S.op("act", lambda e, c=c, ps=ps: e.activation(out=self.scr_bf(2, c), in_=ps[:], func=AF.Gelu_apprx_tanh),
                         r=[f"ps{pi}"], w=[f"scr{2 + c // 2}"])

        def proj_tm(wv, wres, kind):
            for b in range(4):
                pi = b % 4
                ps = self.ps[pi]
                for k in range(8):
                    S.op("pe", lambda e, k=k, b=b, ps=ps: e.matmul(ps[:], lhsT=self.hT[:, k, b * 128:(b + 1) * 128],
                                                                   rhs=wv[:, k, :], start=(k == 0), stop=(k == 7)),
                         r=[wres, "hT"], w=[f"ps{pi}"])
                if kind == "v":
                    S.op("act", lambda e, b=b, ps=ps: e.activation(out=self.v_sb[:, tt * 4 + b, :], in_=ps[:], func=AF.Copy),
                         r=[f"ps{pi}"], w=["v_sb"])
                else:
                    vg = self.scr(0)
                    vg3 = vg.rearrange("p (g d) -> p g d", g=8)
                    tB = self.scr(1).rearrange("p (g d) -> p g d", g=8)
                    S.op("act", lambda e, ps=ps: e.activation(out=vg, in_=ps[:], func=AF.Gelu_apprx_tanh),
                         r=[f"ps{pi}"], w=["scr0"])
                    S.op("dve", lambda e: e.tensor_reduce(out=self.st8[:], in_=vg3, axis=AX.X, op=ALU.add),
                         r=["scr0"], w=["st8"])
                    S.op("dve", lambda e: e.tensor_scalar(out=self.st8[:], in0=self.st8[:], scalar1=-1.0 / 64, scalar2=None,
                                                          op0=ALU.mult), r=["st8"], w=["st8"])
                    S.op("dve", lambda e: e.tensor_tensor(out=vg3, in0=vg3,
                                                          in1=self.st8[:].unsqueeze(2).to_broadcast([128, 8, 64]), op=ALU.add),
                         r=["scr0", "st8"], w=["scr0"])
                    S.op("dve", lambda e: e.tensor_tensor(out=tB, in0=vg3, in1=vg3, op=ALU.mult),
                         r=["scr0"], w=["scr1"])
                    S.op("dve", lambda e: e.tensor_reduce(out=self.st8b[:], in_=tB, axis=AX.X, op=ALU.add),
                         r=["scr1"], w=["st8b"])
                    S.op("act", lambda e: e.activation(out=self.st8b[:], in_=self.st8b[:], func=AF.Ln,
                                                       bias=self.kc[:, 0:1], scale=1.0 / 64),
                         r=["st8b", "kc"], w=["st8b"])
                    S.op("act", lambda e: e.activation(out=self.st8b[:], in_=self.st8b[:], func=AF.Exp, scale=-0.5),
                         r=["st8b"], w=["st8b"])
                    S.op("dve", lambda e, b=b: e.tensor_tensor(
                        out=self.scr_bf(4, b).rearrange("p (g d) -> p g d", g=8), in0=vg3,
                        in1=self.st8b[:].unsqueeze(2).to_broadcast([128, 8, 64]), op=ALU.mult),
                        r=["scr0", "st8b"], w=[f"scr{4 + b // 2}"])

        kinds = ["q", "k", "v", "u", "vg"]
        for kind in ["vg", "u", "q", "k", "v"]:
            gi = kinds.index(kind)
            f = proj_fm if kind in ("q", "k", "u") else proj_tm
            self.item(lambda wv, wr, kind=kind, f=f: f(wv, wr, kind), load=(W[:, :, gi * 512:(gi + 1) * 512], (8, 512)))

        def sgu(wv, wr):
            for c in range(4):
                pi = c % 2
                ps = self.ps[pi]
                S.dma("sp", self.scr(1), dr[f"sgub{j}"][:, c * 512:(c + 1) * 512], w=["scr1"])
                for jj in range(4):
                    for gg in range(2):
                        g = 2 * c + gg
                        S.op("pe", lambda e, g=g, gg=gg, jj=jj, ps=ps: e.matmul(
                            ps[gg * 64:(gg + 1) * 64, jj * 128:(jj + 1) * 128],
                            lhsT=self.scr_bf(4, jj)[:, g * 64:(g + 1) * 64], rhs=self.wsT[:, g, :], start=True, stop=True),
                            r=["scr4", "scr5", "wsT"], w=[f"ps{pi}"])
                S.op("dve", lambda e, ps=ps: e.tensor_tensor(out=self.scr(0), in0=ps[:], in1=self.scr(1), op=ALU.add),
                     r=[f"ps{pi}", "scr1"], w=["scr0"])
                S.op("dve", lambda e, c=c: e.tensor_tensor(out=mixT[:, 4 + c, :], in0=self.scr(0), in1=self.scr_bf(2, c), op=ALU.mult),
                     r=["scr0", f"scr{2 + c // 2}"], w=["hT"])
        self.item(sgu)

        nkb = 4 * (tt + 1)
        def attn_head(wv, wr, h):
            c = h // 2
            r0 = (h % 2) * 64
            qh = self.qT[r0:r0 + 64, c, :]
            for kb in range(nkb):
                pi = 2 + kb % 2
                ps = self.ps[pi]
                ei = kb % 2
                kl = kb - 4 * tt
                c0 = max(kl, 0) * 128
                S.op("pe", lambda e, kb=kb, ps=ps, c0=c0: e.matmul(ps[:, c0:T], lhsT=self.nkT[r0:r0 + 64, c, kb * 128:(kb + 1) * 128],
                                                                   rhs=qh[:, c0:T], start=True, stop=True),
                     r=["nkT", "qT"], w=[f"ps{pi}"])
                S.op("pe", lambda e: e.matmul(self.ps[7][:], lhsT=self.cb(C_GE), rhs=self.cstb[:, C_AM:C_AM + 512], start=True, stop=True),
                     r=["cst"], w=["ps7"])
                S.op("act", lambda e, ps=ps, ei=ei, c0=c0: e.activation(out=self.scr(ei, c0, T), in_=ps[:, c0:T], func=AF.Exp, scale=-1.0),
                     r=[f"ps{pi}"], w=[f"scr{ei}"])
                S.op("act", lambda e, kb=kb, ei=ei, c0=c0: e.activation(out=sp_all[:, kb, c0:T], in_=self.scr(ei, c0, T), func=AF.Ln,
                                                                        bias=self.kc[:, 1:2]),
                     r=[f"scr{ei}", "kc"], w=[f"sp{kb}"])
                if kl >= 0:
                    S.op("dve", lambda e, kb=kb, kl=kl, c0=c0: e.tensor_tensor(
                        out=sp_all[:, kb, c0:T], in0=sp_all[:, kb, c0:T], in1=self.amask(kl)[:, c0:T], op=ALU.mult),
                        r=[f"sp{kb}", "cst"], w=[f"sp{kb}"])
            opi = 6
            ops = self.ps[opi]
            order = list(range(nkb - 1, -1, -1))

            def emitE(idx):
                kb = order[idx]
                pi = 4 + idx % 2
                ps = self.ps[pi]
                last = (kb == nkb - 1)
                if not last:
                    if kb == nkb - 2:
                        S.op("dve", lambda e: e.tensor_copy(out=self.Rb[kb % 2][:], in_=sp_all[:, kb + 1, :]),
                             r=[f"sp{kb + 1}"], w=[f"Rb{kb % 2}"])
                    else:
                        S.op("dve", lambda e: e.tensor_tensor(out=self.Rb[kb % 2][:], in0=self.Rb[(kb + 1) % 2][:],
                                                              in1=sp_all[:, kb + 1, :], op=ALU.add),
                             r=[f"sp{kb + 1}", f"Rb{(kb + 1) % 2}"], w=[f"Rb{kb % 2}"])
                c0 = max(kb - 4 * tt, 0) * 128
                S.op("pe", lambda e: e.matmul(ps[:, c0:T], lhsT=self.cb(C_GE), rhs=sp_all[:, kb, c0:T], start=True, stop=False),
                     r=[f"sp{kb}", "cst"], w=[f"ps{pi}"])
                if not last:
                    S.op("pe", lambda e: e.matmul(ps[:, c0:T], lhsT=self.cb(C_ONE), rhs=self.Rb[kb % 2][:, c0:T], start=False, stop=False),
                         r=[f"Rb{kb % 2}", "cst"], w=[f"ps{pi}"])
                S.op("pe", lambda e: e.matmul(ps[:, c0:T], lhsT=self.nkT[r0:r0 + 64, c, kb * 128:(kb + 1) * 128], rhs=qh[:, c0:T],
                                              start=False, stop=True), r=["nkT", "qT"], w=[f"ps{pi}"])
                for _ in range(2):
                    S.op("pe", lambda e: e.matmul(self.ps[7][:], lhsT=self.cb(C_GE), rhs=self.cstb[:, C_AM:C_AM + 512], start=True, stop=True),
                         r=["cst"], w=["ps7"])

            def emitW(idx):
                kb = order[idx]
                pi = 4 + idx % 2
                ps = self.ps[pi]
                wt = self.w_t[idx % 2]
                wres = f"w_t{idx % 2}"
                c0 = max(kb - 4 * tt, 0) * 128
                S.op("act", lambda e: e.activation(out=wt[:, c0:T], in_=ps[:, c0:T], func=AF.Exp, scale=-1.0), r=[f"ps{pi}"], w=[wres])
                kl = kb - 4 * tt
                if kl >= 0:
                    S.op("dve", lambda e: e.tensor_tensor(out=wt[:], in0=wt[:], in1=self.amask(kl), op=ALU.mult),
                         r=[wres, "cst"], w=[wres])

            def emitPV(idx):
                kb = order[idx]
                wt = self.w_t[idx % 2]
                wres = f"w_t{idx % 2}"
                S.op("pe", lambda e: e.matmul(ops[r0:r0 + 64, :], lhsT=self.v_sb[:, kb, h * 64:(h + 1) * 64], rhs=wt[:],
                                              start=(idx == 0), stop=(idx == nkb - 1)), r=["v_sb", wres], w=["ps6"])

            emitE(0)
            for idx in range(nkb):
                if idx + 1 < nkb:
                    emitE(idx + 1)
                emitW(idx)
                emitPV(idx)
            S.op("act", lambda e: e.activation(out=mixT[r0:r0 + 64, c, :], in_=ops[r0:r0 + 64, :], func=AF.Copy),
                 r=["ps6"], w=["hT"])
        def zero_masked(wv, wr):
            for kl in (1, 2, 3):
                kb = 4 * tt + kl
                S.op("dve", lambda e, kb=kb, kl=kl: e.memset(sp_all[:, kb, 0:kl * 128], 0.0), w=[f"sp{kb}"])
        self.item(zero_masked)
        for h in range(8):
            self.item(lambda wv, wr, h=h: attn_head(wv, wr, h))

        Wo = dr[f"ab_w_out{j}"]
        self.out_proj(lambda n0: Wo[:, :, n0 * 128:(n0 + 2) * 128], 8, lambda k: mixT[:, k, :], ["hT"])
        self.item(lambda wv, wr: self.post_norm_residual(f"mix_post{l}", tt))

    def odd_layer(self, l, j):
        S = self.S
        nc = self.nc
        es2 = ExitStack()
        def sb(name, shape, dt):
            return es2.enter_context(nc.sbuf_tensor(f"{name}_{l}", list(shape), dt))

        def begin(wv, wr):
            self.barrier()
            self._es2 = es2
            self.St = sb("St", (128, 8, 256), F32)
            self.Stb = sb("Stb", (128, 8, 256), BF16)
            self.shalo = sb("shalo", (128, 32, 3), F32)
            self.negA = sb("negA", (128, 32), F32)
            self.zsT = sb("zsT", (128, 16, T), BF16)
            self.dt_tok = sb("dt_tok", (128, 4, 32), F32)
            self.a_b = sb("a_b", (128, 4, 32), BF16)
            self.xp = sb("xp", (128, 2048), BF16)
            self.xd = sb("xd", (128, 256), BF16)
            self.Btok = sb("Btok", (128, 1024), BF16)
            self.aTri2 = [sb(f"aTri{i}", (128, 4, 128), BF16) for i in range(2)]
            self.decay = sb("decay", (128, 32), F32)
            self.cd = sb("cd", (128, 32), F32)
            self.cbm = sb("cbm", (128, 128), BF16)
            self.expD2 = [sb(f"expD{i}", (128, 4, 128), BF16) for i in range(2)]
            self.MT2 = [sb(f"MT{i}", (128, 4, 128), BF16) for i in range(2)]
            self.Eac = sb("Eac", (128, 4, 128), BF16)
            self.Cp2 = [sb(f"Cp{i}", (128, 4, 128), BF16) for i in range(2)]
            self.stmp = sb("stmp", (128, 256), F32)
            self.fhalo = sb("fhalo", (128, 44, 2), F32)
            print("[sbuf] odd layer remaining bytes/partition:", nc.sbuf_bytes_remaining)
            S.op("dve", lambda e: e.memset(self.fhalo[:], 0.0), w=[f"fhalo{c_}" for c_ in range(44)])
            S.op("dve", lambda e: e.memset(self.St[:], 0.0), w=[f"St{g_}" for g_ in range(8)])
            S.op("dve", lambda e: e.memset(self.Stb[:], 0.0), w=[f"Stb{g_}" for g_ in range(8)])
            S.op("dve", lambda e: e.memset(self.shalo[:], 0.0), w=[f"shalo{c_}" for c_ in range(32)])
            S.op("act", lambda e: e.activation(out=self.negA[:], in_=self.rrow(f"alog{j}"), func=AF.Exp), r=["rv"], w=["negA"])
            S.op("dve", lambda e: e.tensor_scalar(out=self.negA[:], in0=self.negA[:], scalar1=-1.0, scalar2=None, op0=ALU.mult),
                 r=["negA"], w=["negA"])
        self.item(begin)
        for tt in range(NT):
            hoist = self.do_mixer and self.do_ffn and tt + 1 < NT
            if self.do_mixer:
                self.ssd_tile(l, j, tt, skip_pre=(hoist_prev if tt > 0 else False))
            if self.do_ffn:
                self.ffn(l, tt, mid=(lambda tt=tt: self.item(lambda wv, wr: self.pre_norm(f"mix_pre{l}", tt + 1))) if hoist else None)
            hoist_prev = hoist
        def end(wv, wr):
            self.barrier()
            es2.close()
            self._es2 = None
        self.item(end)

    def ssd_tile(self, l, j, tt, skip_pre=False):
        S = self.S
        dr = self.dr
        xsT = self.U24
        def BT(g, cs):
            return self.U24[:, 16 + g, cs]
        def CT(g, cs):
            return self.MB[:, g, cs]
        if not skip_pre:
            self.item(lambda wv, wr: self.pre_norm(f"mix_pre{l}", tt))
        W = dr[f"ssd_w_in{j}"]

        def zproj(wv, wres, q):
            for cc in range(4):
                c = q * 4 + cc
                pi = cc % 4
                ps = self.ps[pi]
                for k in range(8):
                    S.op("pe", lambda e, k=k, cc=cc, ps=ps: e.matmul(ps[:], lhsT=wv[:, k, cc * 128:(cc + 1) * 128], rhs=self.hT[:, k, :],
                                                                     start=(k == 0), stop=(k == 7)), r=[wres, "hT"], w=[f"ps{pi}"])
                S.op("act", lambda e, c=c, ps=ps: e.activation(out=self.zsT[:, c, :], in_=ps[:], func=AF.Silu),
                     r=[f"ps{pi}"], w=["zsT"])

        def xbcproj(wv, wres, q):
            pending = []
            for cc in range(4):
                ch = q * 4 + cc
                pi = cc % 4
                ps = self.ps[pi]
                for k in range(8):
                    S.op("pe", lambda e, k=k, cc=cc, ps=ps: e.matmul(ps[:], lhsT=wv[:, k, cc * 128:(cc + 1) * 128], rhs=self.hT[:, k, :],
                                                                     start=(k == 0), stop=(k == 7)), r=[wres, "hT"], w=[f"ps{pi}"])
                si = 2 * (ch % 3)
                ai = si + 1
                S.op("pool", lambda e, ch=ch, si=si: e.tensor_copy(out=self.SCR[:, si, 0:3], in_=self.shalo[:, ch, :]), r=[f"shalo{ch}"], w=[f"scr{si}h"])
                S.op("act", lambda e, ps=ps, si=si: e.activation(out=self.SCR[:, si, 3:3 + T], in_=ps[:], func=AF.Copy), r=[f"ps{pi}"], w=[f"scr{si}"])
                S.op("pool", lambda e, ch=ch, si=si: e.tensor_copy(out=self.shalo[:, ch, :], in_=self.SCR[:, si, T:T + 3]), r=[f"scr{si}"], w=[f"shalo{ch}"])
                S.op("act", lambda e, ch=ch, si=si, ai=ai: e.activation(out=self.scr(ai), in_=self.SCR[:, si, 0:T], func=AF.Identity,
                                                                        bias=self.pcol(f"scb{j}", ch), scale=self.pcol(f"scw{j}_0", ch)),
                     r=[f"scr{si}", f"scr{si}h", "pv"], w=[f"scr{ai}"])
                for f in pending:
                    f()
                pending = []
                for kk in (1, 2, 3):
                    S.op("dve", lambda e, ch=ch, kk=kk, si=si, ai=ai: e.scalar_tensor_tensor(
                        out=self.scr(ai), in0=self.SCR[:, si, kk:kk + T], scalar=self.pcol(f"scw{j}_{kk}", ch), in1=self.scr(ai),
                        op0=ALU.mult, op1=ALU.add), r=[f"scr{si}", f"scr{si}h", "pv", f"scr{ai}"], w=[f"scr{ai}"])
                if ch < 16:
                    dst, dres = xsT[:, ch, :], "xsT"
                elif ch < 24:
                    dst, dres = BT(ch - 16, slice(0, T)), "BT"
                else:
                    dst, dres = CT(ch - 24, slice(0, T)), "m_sb"
                pending.append(lambda dst=dst, ai=ai, dres=dres: S.op(
                    "act", lambda e: e.activation(out=dst, in_=self.scr(ai), func=AF.Silu), r=[f"scr{ai}"], w=[dres]))
            for f in pending:
                f()
        for q in range(8):
            if q % 2 == 0:
                qz = q // 2
                self.item(lambda wv, wr, qz=qz: zproj(wv, wr, qz), load=(W[:, :, qz * 512:(qz + 1) * 512], (8, 512)))
            self.item(lambda wv, wr, q=q: xbcproj(wv, wr, q), load=(W[:, :, 2048 + q * 512:2048 + (q + 1) * 512], (8, 512)))

        def dtproj(wv, wres):
            ps = self.ps[0]
            for b in range(4):
                for k in range(8):
                    S.op("pe", lambda e, k=k, b=b: e.matmul(ps[:, b * 32:(b + 1) * 32], lhsT=self.hT[:, k, b * 128:(b + 1) * 128],
                                                            rhs=wv[:, k, :], start=(k == 0), stop=(k == 7)),
                         r=[wres, "hT"], w=["ps0"])
            d3 = self.dt_tok[:]
            S.op("dve", lambda e: e.tensor_tensor(out=d3, in0=ps[:, 0:128].rearrange("p (b h) -> p b h", b=4),
                                                  in1=self.rrow(f"dtb{j}").unsqueeze(1).to_broadcast([128, 4, 32]), op=ALU.add),
                 r=["ps0", "rv"], w=["dt_tok"])
            S.op("act", lambda e: e.activation(out=d3, in_=d3, func=AF.Exp), r=["dt_tok"], w=["dt_tok"])
            S.op("act", lambda e: e.activation(out=d3, in_=d3, func=AF.Ln, bias=self.kc[:, 1:2]), r=["dt_tok", "kc"], w=["dt_tok"])
            S.op("dve", lambda e: e.tensor_tensor(out=self.a_b[:], in0=d3,
                                                  in1=self.negA[:].unsqueeze(1).to_broadcast([128, 4, 32]), op=ALU.mult),
                 r=["dt_tok", "negA"], w=["a_b"])
        self.item(dtproj, load=(W[:, :, 6144:6176], (8, 32)))

        def chunk(wv, wr, jc):
            cs = slice(jc * 128, (jc + 1) * 128)
            ygall = self.MBf[:, 2048:4096].rearrange("p (c l) -> p c l", c=16)
            idb = self.cb(C_ID)
            for bq in range(4):
                for cc in range(4):
                    c = bq * 4 + cc
                    S.op("pe", lambda e, c=c, cc=cc: e.matmul(self.ps[7][:, cc * 128:(cc + 1) * 128], lhsT=xsT[:, c, cs], rhs=idb,
                                                              start=True, stop=True),
                         r=["xsT", "cst"], w=["ps7"])
                S.op("dve", lambda e, bq=bq: e.tensor_tensor(
                    out=self.xp[:, bq * 512:(bq + 1) * 512].rearrange("p (h d) -> p h d", h=8),
                    in0=self.ps[7][:].rearrange("p (h d) -> p h d", h=8),
                    in1=self.dt_tok[:, jc, bq * 8:(bq + 1) * 8].unsqueeze(2).to_broadcast([128, 8, 64]), op=ALU.mult),
                    r=["ps7", "dt_tok"], w=["xp"])
            if getattr(self, "cut", 99) <= 1:
                return
            for half in range(2):
                for gg in range(4):
                    g = half * 4 + gg
                    S.op("pe", lambda e, g=g, gg=gg: e.matmul(self.ps[6][:, gg * 128:(gg + 1) * 128], lhsT=BT(g, cs), rhs=idb,
                                                              start=True, stop=True),
                         r=["BT", "cst"], w=["ps6"])
                S.op("act", lambda e, half=half: e.activation(out=self.Btok[:, half * 512:(half + 1) * 512], in_=self.ps[6][:], func=AF.Copy),
                     r=["ps6"], w=["Btok"])
            if getattr(self, "cut", 99) <= 2:
                return
            ps0 = self.ps[0]
            S.op("pe", lambda e: e.matmul(ps0[:, 0:32], lhsT=self.cb(C_GT), rhs=self.a_b[:, jc, :], start=True, stop=True),
                 r=["cst", "a_b"], w=["ps0"])
            S.op("pe", lambda e: e.matmul(ps0[:, 32:64], lhsT=self.cb(C_ONE), rhs=self.a_b[:, jc, :], start=True, stop=True),
                 r=["cst", "a_b"], w=["ps0"])
            S.op("act", lambda e: e.activation(out=self.decay[:], in_=ps0[:, 0:32], func=AF.Exp), r=["ps0"], w=["decay"])
            S.op("act", lambda e: e.activation(out=self.cd[:], in_=ps0[:, 32:64], func=AF.Exp), r=["ps0"], w=["cd"])
            if getattr(self, "cut", 99) <= 3:
                return
            def head(g):
                h0 = 4 * g
                p = g % 2
                aTri, expD, MT, Cp = self.aTri2[p], self.expD2[p], self.MT2[p], self.Cp2[p]
                bD, bA = 2, 3
                bY, yo, yres = (4, 0, 'ps4') if p == 0 else (0, 0, 'ps0')
                bS = 5 + p
                S.op("pool", lambda e: e.tensor_tensor(
                    out=aTri[:], in0=self.cb(C_LE).unsqueeze(1).to_broadcast([128, 4, 128]),
                    in1=self.a_b[:, jc, h0:h0 + 4].unsqueeze(2).to_broadcast([128, 4, 128]), op=ALU.mult),
                    r=["cst", "a_b"], w=[f"aTri{p}"])
                S.op("pool", lambda e: e.tensor_tensor(
                    out=self.xd[:].rearrange("p (h d) -> p h d", h=4),
                    in0=self.xp[:, g * 256:(g + 1) * 256].rearrange("p (h d) -> p h d", h=4),
                    in1=self.decay[:, h0:h0 + 4].unsqueeze(2).to_broadcast([128, 4, 64]), op=ALU.mult),
                    r=["xp", "decay"], w=["xd"])
                aT2 = aTri[:].rearrange("p h l -> p (h l)")
                ps1 = self.ps[1]
                S.op("pe", lambda e: e.matmul(ps1[:, 0:128], lhsT=BT(g, cs), rhs=CT(g, cs), start=True, stop=True),
                     r=["BT", "m_sb"], w=["ps1"])
                S.op("dve", lambda e: e.tensor_tensor(out=self.cbm[:], in0=ps1[:, 0:128], in1=self.cb(C_LE), op=ALU.mult),
                     r=["ps1", "cst"], w=["cbm"])
                psD = self.ps[bD]
                S.op("pe", lambda e: e.matmul(psD[:], lhsT=self.cb(C_GT), rhs=aT2, start=True, stop=True), r=["cst", f"aTri{p}"], w=[f"ps{bD}"])
                S.op("act", lambda e: e.activation(out=expD[:].rearrange("p h l -> p (h l)"), in_=psD[:], func=AF.Exp),
                     r=[f"ps{bD}"], w=[f"expD{p}"])
                S.op("dve", lambda e: e.tensor_tensor(out=MT[:], in0=expD[:],
                                                      in1=self.cbm[:].unsqueeze(1).to_broadcast([128, 4, 128]), op=ALU.mult),
                     r=[f"expD{p}", "cbm"], w=[f"MT{p}"])
                psA = self.ps[bA]
                S.op("pe", lambda e: e.matmul(psA[:], lhsT=self.cb(C_ONE), rhs=aT2, start=True, stop=True), r=["cst", f"aTri{p}"], w=[f"ps{bA}"])
                S.op("act", lambda e: e.activation(out=self.Eac[:].rearrange("p h l -> p (h l)"), in_=psA[:], func=AF.Exp),
                     r=[f"ps{bA}"], w=["Eac"])
                S.op("pool", lambda e: e.tensor_tensor(out=Cp[:], in0=self.Eac[:],
                                                       in1=CT(g, cs).unsqueeze(1).to_broadcast([128, 4, 128]), op=ALU.mult),
                     r=["Eac", "m_sb"], w=[f"Cp{p}"])
                psY = self.ps[bY]
                for r_ in range(4):
                    h = h0 + r_
                    cl = r_ // 2
                    ro = (h % 2) * 64
                    S.op("pe", lambda e, h=h, r_=r_, cl=cl, ro=ro: e.matmul(
                        psY[ro:ro + 64, yo + cl * 128:yo + (cl + 1) * 128], lhsT=self.xp[:, h * 64:(h + 1) * 64], rhs=MT[:, r_, :],
                        start=True, stop=False), r=["xp", f"MT{p}"], w=[yres])
                    S.op("pe", lambda e, r_=r_, cl=cl, ro=ro: e.matmul(
                        psY[ro:ro + 64, yo + cl * 128:yo + (cl + 1) * 128], lhsT=self.Stb[:, g, r_ * 64:(r_ + 1) * 64], rhs=Cp[:, r_, :],
                        start=False, stop=True), r=[f"Stb{g}", f"Cp{p}"], w=[yres])
                ps5 = self.ps[bS]
                S.op("pe", lambda e: e.matmul(ps5[:, 0:256], lhsT=self.Btok[:, g * 128:(g + 1) * 128], rhs=self.xd[:],
                                              start=True, stop=True), r=["Btok", "xd"], w=[f"ps{bS}"])

            def tail(g):
                h0 = 4 * g
                p = g % 2
                bY, yo, yres = (4, 0, 'ps4') if p == 0 else (0, 0, 'ps0')
                psY = self.ps[bY]
                bS = 5 + p
                ps5 = self.ps[bS]
                S.op("pool", lambda e: e.tensor_tensor(
                    out=self.stmp[:].rearrange("p (r d) -> p r d", r=4), in0=self.St[:, g, :].rearrange("p (r d) -> p r d", r=4),
                    in1=self.cd[:, h0:h0 + 4].unsqueeze(2).to_broadcast([128, 4, 64]), op=ALU.mult),
                    r=[f"St{g}", "cd"], w=["stmp"])
                S.op("dve", lambda e: e.tensor_tensor(out=self.St[:, g, :], in0=self.stmp[:], in1=ps5[:, 0:256], op=ALU.add),
                     r=["stmp", f"ps{bS}"], w=[f"St{g}"])
                S.op("act", lambda e: e.activation(out=self.Stb[:, g, :], in_=self.St[:, g, :], func=AF.Copy), r=[f"St{g}"], w=[f"Stb{g}"])
                for cl in range(2):
                    c = 2 * g + cl
                    S.op("dve", lambda e, c=c, cl=cl: e.scalar_tensor_tensor(
                        out=ygall[:, c, :], in0=xsT[:, c, cs], scalar=self.pcol(f"sd{j}", c), in1=psY[:, yo + cl * 128:yo + (cl + 1) * 128],
                        op0=ALU.mult, op1=ALU.add), r=["xsT", "pv", yres], w=["ygall"])
                    S.op("dve", lambda e, c=c, cl=cl: e.tensor_tensor(out=ygall[:, c, :], in0=ygall[:, c, :], in1=self.zsT[:, c, cs], op=ALU.mult),
                         r=["ygall", "zsT"], w=["ygall"])

            head(0)
            for g in range(1, 8):
                head(g)
                tail(g - 1)
            tail(7)
            for q in range(4):
                b = q % 2
                S.op("act", lambda e, q=q, b=b: e.activation(out=self.sq[:, b, :], in_=ygall[:, 4 * q:4 * q + 4, :].rearrange("p c l -> p (c l)"),
                                                             func=AF.Square), r=["ygall"], w=[f"sq{b}"])
                for cc in range(4):
                    c = 4 * q + cc
                    g = c // 2
                    pi = 6 + g // 4
                    S.op("pe", lambda e, cc=cc, b=b, g=g, pi=pi, c=c: e.matmul(
                        self.ps[pi][:, (g % 4) * 128:(g % 4 + 1) * 128], lhsT=self.cb(C_ONE), rhs=self.sq[:, b, cc * 128:(cc + 1) * 128],
                        start=(c % 2 == 0), stop=(c % 2 == 1)), r=[f"sq{b}", "cst"], w=[f"ps{pi}"])
            for hf in range(2):
                S.op("act", lambda e, hf=hf: e.activation(out=self.scr(4 + hf), in_=self.ps[6 + hf][:], func=AF.Ln,
                                                          bias=self.kc[:, 0:1], scale=1.0 / 256), r=[f"ps{6 + hf}", "kc"], w=[f"scr{4 + hf}"])
                S.op("act", lambda e, hf=hf: e.activation(out=self.scr(4 + hf), in_=self.scr(4 + hf), func=AF.Exp, scale=-0.5),
                     r=[f"scr{4 + hf}"], w=[f"scr{4 + hf}"])
                S.op("dve", lambda e, hf=hf: e.tensor_tensor(
                    out=ygall[:, 8 * hf:8 * hf + 8, :].rearrange("p (g two) l -> p g two l", two=2),
                    in0=ygall[:, 8 * hf:8 * hf + 8, :].rearrange("p (g two) l -> p g two l", two=2),
                    in1=self.scr(4 + hf).rearrange("p (g l) -> p g l", g=4).unsqueeze(2).to_broadcast([128, 4, 2, 128]), op=ALU.mult),
                    r=["ygall", f"scr{4 + hf}"], w=["ygall"])
            o_ng, _ = PCOLS[f"sng{j}"]
            S.op("dve", lambda e: e.tensor_tensor(out=self.zsT[:, :, cs], in0=ygall[:],
                                                  in1=self.pv[:, o_ng:o_ng + 16].unsqueeze(2).to_broadcast([128, 16, 128]), op=ALU.mult),
                 r=["ygall", "pv"], w=["zsT"])
        for jc in range(4):
            self.item(lambda wv, wr, jc=jc: chunk(wv, wr, jc))
        Wo = dr[f"ssd_w_out{j}"]
        self.out_proj(lambda n0: Wo[:, :, n0 * 128:(n0 + 2) * 128], 16, lambda k: self.zsT[:, k, :], ["zsT"])
        self.item(lambda wv, wr: self.post_norm_residual(f"mix_post{l}", tt))


def _pk(v, w):
    return np.ascontiguousarray(np.asarray(v, np.float32).reshape(w, 128).T)


def _rows(w, nk):
    w = np.asarray(w, np.float32)
    return np.ascontiguousarray(w.reshape(nk, 128, w.shape[1]).transpose(1, 0, 2))


def prepare_shared(inp, n_layers=DEPTH):
    pv = np.zeros((128, NPV), np.float32)
    def put(name, arr):
        o, w = PCOLS[name]
        assert arr.shape == (128, w), (name, arr.shape)
        pv[:, o:o + w] = arr
    for l in range(DEPTH):
        put(f"mix_pre{l}", _pk(inp["mix_pre_g"][l], 8))
        put(f"mix_post{l}", _pk(inp["mix_post_g"][l], 8))
        put(f"ffn_pre{l}", _pk(inp["ffn_pre_g"][l], 8))
        put(f"ffn_post{l}", _pk(inp["ffn_post_g"][l], 8))
        for k in range(3):
            put(f"fcw{l}_{k}", _pk(inp["ffn_conv_w"][l][k], 44))
        put(f"fcb{l}", _pk(inp["ffn_conv_b"][l], 44))
    for j in range(2):
        for k in range(4):
            put(f"scw{j}_{k}", _pk(inp["ssd_conv_w"][j][k], 32))
        put(f"scb{j}", _pk(inp["ssd_conv_b"][j], 32))
        put(f"sng{j}", _pk(inp["ssd_norm_g"][j], 16))
        put(f"sd{j}", _pk(np.repeat(np.asarray(inp["ssd_d"][j], np.float32), 64), 16))
    rv = np.zeros((128, NRV), np.float32)
    sgub = {}
    for j in range(2):
        b = np.asarray(inp["sgu_b"][j], np.float32)
        blk = np.zeros((128, 4, 4, 128), np.float32)
        for c in range(4):
            blk[0:64, c, :, :] = b[2 * c][None, None, :]
            blk[64:128, c, :, :] = b[2 * c + 1][None, None, :]
        sgub[j] = blk.reshape(128, 2048)
        o, w = RCOLS[f"dtb{j}"]
        rv[:, o:o + w] = np.asarray(inp["ssd_dt_bias"][j], np.float32)[None, :]
        o, w = RCOLS[f"alog{j}"]
        rv[:, o:o + w] = np.asarray(inp["ssd_a_log"][j], np.float32)[None, :]
    m = {"pvec": pv, "rvec": rv, "cst": _consts()}
    for j in range(2):
        m[f"ab_w_in{j}"] = _rows(inp["ab_w_in"][j], 8)
        m[f"ab_w_out{j}"] = _rows(inp["ab_w_out"][j], 8)
        m[f"sgub{j}"] = sgub[j]
        m[f"sgu_wT{j}"] = np.ascontiguousarray(np.asarray(inp["sgu_w"][j], np.float32).transpose(2, 0, 1))
        m[f"ssd_w_in{j}"] = _rows(inp["ssd_w_in"][j], 8)
        m[f"ssd_w_out{j}"] = _rows(inp["ssd_w_out"][j], 16)
    for l in range(DEPTH):
        wu = _rows(inp["ffn_w_up"][l], 8)
        g = wu[:, :, :DFF].reshape(128, 8, NPAIR, 128)
        v = wu[:, :, DFF:].reshape(128, 8, NPAIR, 128)
        m[f"ffn_w_up{l}"] = np.ascontiguousarray(np.concatenate([g, v], axis=3).transpose(0, 2, 1, 3))
        wd = _rows(inp["ffn_w_down"][l], NPAIR)
        m[f"ffn_w_down{l}"] = np.ascontiguousarray(wd.reshape(128, NPAIR, 8, 128).transpose(0, 2, 1, 3))
    return m


def x_to_core(xb):
    return np.ascontiguousarray(np.asarray(xb, np.float32).T.reshape(8, 128, SEQ).transpose(1, 0, 2).reshape(128, 8 * SEQ))


def core_to_x(y):
    return np.ascontiguousarray(y.reshape(128, 8, SEQ).transpose(1, 0, 2).reshape(D, SEQ).T)


_NC_CACHE = {}


def run(inputs, n_layers=DEPTH, do_mixer=True, do_ffn=True, ncores=NB, trace=False):
    x = np.asarray(inputs["x"], np.float32)
    shared = prepare_shared(inputs)
    key = (n_layers, do_mixer, do_ffn)
    if key not in _NC_CACHE:
        _NC_CACHE[key] = Builder(n_layers, do_mixer, do_ffn).build()
    nc = _NC_CACHE[key]
    in_maps = []
    for b in range(ncores):
        mm = dict(shared)
        mm["xT"] = x_to_core(x[b])
        in_maps.append(mm)
    res = run_bass_kernel_spmd(nc, in_maps, core_ids=list(range(ncores)), **({"trace": True} if trace else {}))
    out = np.stack([core_to_x(np.asarray(r["yT"])) for r in res.results], axis=0)
    if trace:
        print("exec_time_ns", res.exec_time_ns)
    return out.astype(np.float32)


def kernel(**inputs):
    return run(inputs)
```

```python
import numpy as np
from contextlib import ExitStack
import concourse.bass as bass
import concourse.mybir as mybir
from concourse.bass_utils import run_bass_kernel_spmd

F32 = mybir.dt.float32
BF16 = mybir.dt.bfloat16
AF = mybir.ActivationFunctionType
ALU = mybir.AluOpType
AX = mybir.AxisListType

D = 1024
SEQ = 2048
NB = 8
DEPTH = 4
T = 512
NT = SEQ // T
DFF = 2816
NPAIR = DFF // 128
AB_IN = 2560
SSD_IN = 6176
EPS = 1e-6

def _build_pcols():
    cols = {}
    n = 0
    def add(name, w):
        nonlocal n
        cols[name] = (n, w)
        n += w
    for l in range(DEPTH):
        for k in ("mix_pre", "mix_post", "ffn_pre", "ffn_post"):
            add(f"{k}{l}", 8)
        for k in range(3):
            add(f"fcw{l}_{k}", 44)
        add(f"fcb{l}", 44)
    for j in range(2):
        for k in range(4):
            add(f"scw{j}_{k}", 32)
        add(f"scb{j}", 32)
        add(f"sng{j}", 16)
        add(f"sd{j}", 16)
    return cols, n

PCOLS, NPV = _build_pcols()

RCOLS = {}
_n = 0
for j in range(2):
    RCOLS[f"dtb{j}"] = (_n, 32); _n += 32
    RCOLS[f"alog{j}"] = (_n, 32); _n += 32
NRV = _n

C_ID, C_ONE, C_GE, C_LE, C_GT, C_AM = 0, 128, 256, 384, 512, 640
NCST = 640 + 896


def _consts():
    a = np.arange(128)[:, None]
    b = np.arange(128)[None, :]
    c = np.zeros((128, NCST), np.float32)
    c[:, C_ID:C_ID + 128] = (a == b)
    c[:, C_ONE:C_ONE + 128] = 1.0
    c[:, C_GE:C_GE + 128] = (a >= b)
    c[:, C_LE:C_LE + 128] = (a <= b)
    c[:, C_GT:C_GT + 128] = (a > b)
    u = np.arange(896)[None, :]
    c[:, C_AM:C_AM + 896] = (a < (u - 384))
    return c


class Sync:
    NS = 8
    SAME_GAP = 6

    def __init__(self, nc, es):
        self.nc = nc
        self.engs = {"pe": nc.tensor, "act": nc.scalar, "dve": nc.vector, "pool": nc.gpsimd, "sp": nc.sync}
        self.semh = {}
        for e in self.engs:
            self.semh[e] = es.enter_context(nc.semaphore(f"s_{e}"))
        for q in ("sp", "pool"):
            for i in range(self.NS):
                self.semh[("dma", q, i)] = es.enter_context(nc.semaphore(f"d_{q}{i}"))
        self.cnt = {e: 0 for e in self.engs}
        self.known = {e: {} for e in self.engs}
        self.lastw = {}
        self.readers = {}
        self.dma_k = {"sp": 0, "pool": 0}
        self.n_wait = 0
        self.n_ins = 0
        self._cur_wide = False
        self.wide_tk = set()

    def need(self, e, tk):
        if tk is None:
            return
        key, val = tk
        if key == e and e != "pool" and (e == "pe" or self.cnt[e] - val >= self.SAME_GAP):
            return
        if key == e and e != "pool" and self._cur_wide and (e, val) in self.wide_tk:
            return
        if self.known[e].get(key, 0) >= val:
            return
        self.engs[e].wait_ge(self.semh[key], val)
        self.known[e][key] = val
        self.n_wait += 1

    def _deps(self, e, r, w):
        for res in r:
            self.need(e, self.lastw.get(res))
        for res in w:
            self.need(e, self.lastw.get(res))
            for k, v in self.readers.get(res, {}).items():
                if k != e or e == "pool":
                    self.need(e, (k, v))

    def _record(self, tk, r, w):
        key, val = tk
        for res in r:
            self.readers.setdefault(res, {})[key] = val
        for res in w:
            self.lastw[res] = tk
            self.readers[res] = {}

    WIDE = ("scr", "m_sb", "hT", "x0", "x1", "x2", "x3", "actT", "sp", "Rb", "w_t", "mbs")

    def op(self, e, fn, r=(), w=()):
        self._cur_wide = len(w) > 0 and all(res.startswith(self.WIDE) for res in w)
        self._deps(e, r, w)
        ins = fn(self.engs[e])
        self.cnt[e] += 1
        ins.then_inc(self.semh[e], 1)
        if self._cur_wide:
            self.wide_tk.add((e, self.cnt[e]))
        self._cur_wide = False
        self._record((e, self.cnt[e]), r, w)
        self.n_ins += 1

    def dma(self, q, out, in_, r=(), w=()):
        k = self.dma_k[q]
        self.dma_k[q] = k + 1
        key = ("dma", q, k % self.NS)
        val = 16 * (k // self.NS + 1)
        if k >= self.NS:
            self.need(q, (key, val - 16))
        self._deps(q, r, w)
        ins = self.engs[q].dma_start(out=out, in_=in_)
        ins.then_inc(self.semh[key], 16)
        tk = (key, val)
        self._record(tk, r, w)
        self.n_ins += 1
        return tk


class Builder:
    def __init__(self, n_layers=DEPTH, do_mixer=True, do_ffn=True):
        self.n_layers = n_layers
        self.do_mixer = do_mixer
        self.do_ffn = do_ffn

    def pcol(self, name, i=0, w=1):
        o, _ = PCOLS[name]
        return self.pv[:, o + i:o + i + w]

    def rrow(self, name, a=0, b=None):
        o, wd = RCOLS[name]
        if b is None:
            b = wd
        return self.rv[:, o + a:o + b]

    def cb(self, off, w=128):
        return self.cstb[:, off:off + w]

    def amask(self, kl):
        o = C_AM + (3 - kl) * 128
        return self.cstb[:, o:o + 512]

    def scr(self, i, a=0, b=T):
        return self.SCR[:, i, a:b]

    def ftile(self, k):
        if k < 6:
            return self.SCR[:, k, :], f"scr{k}"
        o = (k - 6) * (T + 4)
        assert o + T + 4 <= 4096
        return self.MBf[:, o:o + T + 4], f"mbs{k - 6}"

    def scr_bf(self, i0, c):
        i = i0 + c // 2
        v = self.SCR[:, i, 0:512].bitcast(BF16)
        return v[:, (c % 2) * 512:(c % 2 + 1) * 512]

    def build(self):
        nc = bass.Bass("TRN2", target_bir_lowering=False)
        self.nc = nc
        dr = {}
        def din(name, shape):
            dr[name] = nc.dram_tensor(name, list(shape), F32, kind="ExternalInput").ap()
        din("xT", (128, 8 * SEQ))
        din("pvec", (128, NPV))
        din("rvec", (128, NRV))
        din("cst", (128, NCST))
        for j in range(2):
            din(f"ab_w_in{j}", (128, 8, AB_IN))
            din(f"ab_w_out{j}", (128, 8, D))
            din(f"sgu_wT{j}", (128, 8, 128))
            din(f"sgub{j}", (128, 2048))
            din(f"ssd_w_in{j}", (128, 8, SSD_IN))
            din(f"ssd_w_out{j}", (128, 16, D))
        for l in range(DEPTH):
            din(f"ffn_w_up{l}", (128, NPAIR, 8, 256))
            din(f"ffn_w_down{l}", (128, 8, NPAIR, 128))
        dr["yT"] = nc.dram_tensor("yT", [128, 8 * SEQ], F32, kind="ExternalOutput").ap()
        self.dr = dr

        with ExitStack() as es:
            self.es = es
            S = Sync(nc, es)
            self.S = S
            def sb(name, shape, dt):
                return es.enter_context(nc.sbuf_tensor(name, list(shape), dt))
            self.xT = sb("xT_sb", (128, 8, SEQ), F32)
            self.pv = sb("pv", (128, NPV), F32)
            self.rv = sb("rv", (128, NRV), F32)
            self.cstb = sb("cstb", (128, NCST), BF16)
            self.kc = sb("kc", (128, 2), F32)
            self.NWB = 3
            self.wb = [sb(f"wb{i}", (128, 4096), BF16) for i in range(self.NWB)]
            self.ps = [es.enter_context(nc.psum_tensor(f"ps{i}", [128, 512], F32)) for i in range(8)]
            self.sq = sb("sq", (128, 2, T), BF16)
            self.rstd = sb("rstd", (128, T), F32)
            self.hT = sb("hT", (128, 8, T), BF16)
            self.MB = sb("MB", (128, 16, T), BF16)
            self.m_sb = self.MB[:].bitcast(F32).rearrange("p a b -> p (a b)").rearrange("p (c t) -> p c t", c=8)
            self.SCR = sb("SCR", (128, 6, T + 4), F32)
            self.MBf = self.MB[:].bitcast(F32).rearrange("p a b -> p (a b)")
            self.U24 = sb("U24", (128, 24, T), BF16)

            S.dma("sp", self.pv[:], dr["pvec"][:, :], w=["pv"])
            S.dma("sp", self.rv[:], dr["rvec"][:, :], w=["rv"])
            S.dma("pool", self.cstb[:], dr["cst"][:, :], w=["cst"])
            S.op("dve", lambda e: e.memset(self.kc[:, 0:1], EPS), w=["kc"])
            S.op("dve", lambda e: e.memset(self.kc[:, 1:2], 1.0), w=["kc"])
            for c in range(8):
                S.dma("sp", self.xT[:, c, :], dr["xT"][:, c * SEQ:(c + 1) * SEQ], w=[f"x{tt}" for tt in range(NT)])

            self.items = []
            for l in (getattr(self, "layer_list", None) or range(self.n_layers)):
                j = l // 2
                if l % 2 == 0:
                    self.even_layer(l, j)
                else:
                    self.odd_layer(l, j)
            if getattr(self, "max_items", None):
                self.items = self.items[:self.max_items]
            self.run_items()
            dbg_tks = []
            if getattr(self, "dbg", None):
                self.barrier()
                for name, apf, shape, dt in self.dbg:
                    o = nc.dram_tensor(name, list(shape), dt, kind="ExternalOutput").ap()
                    dbg_tks.append(S.dma("sp", o, apf(self)))
            for tk in dbg_tks:
                S.need("sp", tk)

            tks = []
            for c in range(8):
                tks.append(S.dma("sp", dr["yT"][:, c * SEQ:(c + 1) * SEQ], self.xT[:, c, :],
                                 r=[f"x{tt}" for tt in range(NT)]))
            for tk in tks:
                S.need("sp", tk)
            if getattr(self, "_es2", None):
                self._es2.close()
            print(f"[build] instructions={S.n_ins} waits={S.n_wait}")
        return nc

    def item(self, fn, load=None):
        self.items.append((fn, load))

    def run_items(self):
        S = self.S
        loads = [i for i, (fn, ld) in enumerate(self.items) if ld is not None]
        views = {}
        ptr = 0
        consumed = 0
        for idx, (fn, ld) in enumerate(self.items):
            while ptr < len(loads) and ptr - consumed < self.NWB:
                li = loads[ptr]
                src, shape = self.items[li][1]
                bi = ptr % self.NWB
                buf = self.wb[bi]
                n = int(np.prod(shape))
                assert n <= 4096, shape
                v = buf[:, 0:n]
                if len(shape) == 2:
                    v = v.rearrange("p (a b) -> p a b", a=shape[0])
                elif len(shape) == 3:
                    v = v.rearrange("p (a b c) -> p a b c", a=shape[0], b=shape[1])
                S.dma("pool", v, src, w=[f"wb{bi}"])
                views[li] = (v, f"wb{bi}")
                ptr += 1
            if ld is not None:
                v, res = views.pop(idx)
                fn(v, res)
                consumed += 1
            else:
                fn(None, None)

    def sumsq_rstd(self, srcs, src_res, nfeat, width=T, ps_i=6):
        S = self.S
        n = len(srcs)
        ps = self.ps[ps_i]
        for c, s in enumerate(srcs):
            b = c % 2
            S.op("act", lambda e, b=b, s=s: e.activation(out=self.sq[:, b, 0:width], in_=s, func=AF.Square),
                 r=src_res, w=[f"sq{b}"])
            S.op("pe", lambda e, c=c, b=b: e.matmul(ps[:, 0:width], lhsT=self.cb(C_ONE), rhs=self.sq[:, b, 0:width],
                                                    start=(c == 0), stop=(c == n - 1)),
                 r=[f"sq{b}", "cst"], w=[f"ps{ps_i}"])
        S.op("act", lambda e: e.activation(out=self.rstd[:, 0:width], in_=ps[:, 0:width], func=AF.Ln,
                                           bias=self.kc[:, 0:1], scale=1.0 / nfeat),
             r=[f"ps{ps_i}", "kc"], w=["rstd"])
        S.op("act", lambda e: e.activation(out=self.rstd[:, 0:width], in_=self.rstd[:, 0:width], func=AF.Exp, scale=-0.5),
             r=["rstd"], w=["rstd"])

    def pre_norm(self, gname, tt):
        S = self.S
        ts = slice(tt * T, (tt + 1) * T)
        self.sumsq_rstd([self.xT[:, c, ts] for c in range(8)], [f"x{tt}"], D)
        for c in range(8):
            S.op("dve", lambda e, c=c: e.scalar_tensor_tensor(
                out=self.hT[:, c, :], in0=self.xT[:, c, ts], scalar=self.pcol(gname, c), in1=self.rstd[:],
                op0=ALU.mult, op1=ALU.mult), r=[f"x{tt}", "rstd", "pv"], w=["hT"])

    def post_norm_residual(self, gname, tt):
        S = self.S
        ts = slice(tt * T, (tt + 1) * T)
        self.sumsq_rstd([self.m_sb[:, c, :] for c in range(8)], ["m_sb"], D)
        for c in range(8):
            S.op("dve", lambda e, c=c: e.scalar_tensor_tensor(
                out=self.m_sb[:, c, :], in0=self.m_sb[:, c, :], scalar=self.pcol(gname, c), in1=self.rstd[:],
                op0=ALU.mult, op1=ALU.mult), r=["m_sb", "rstd", "pv"], w=["m_sb"])
            S.op("dve", lambda e, c=c: e.tensor_tensor(out=self.xT[:, c, ts], in0=self.xT[:, c, ts],
                                                       in1=self.m_sb[:, c, :], op=ALU.add),
                 r=["m_sb", f"x{tt}"], w=[f"x{tt}"])

    def out_proj(self, w_ap_fn, nk, src_fn, src_res):
        S = self.S
        for n0 in range(0, 8, 2):
            def fn(wv, wres, n0=n0):
                for dn in range(2):
                    n = n0 + dn
                    pi = n % 2
                    ps = self.ps[pi]
                    for k in range(nk):
                        S.op("pe", lambda e, k=k, dn=dn, ps=ps: e.matmul(
                            ps[:], lhsT=wv[:, k, dn * 128:(dn + 1) * 128], rhs=src_fn(k),
                            start=(k == 0), stop=(k == nk - 1)), r=[wres] + src_res, w=[f"ps{pi}"])
                    S.op("act", lambda e, n=n, ps=ps: e.activation(out=self.m_sb[:, n, :], in_=ps[:], func=AF.Copy),
                         r=[f"ps{pi}"], w=["m_sb"])
            self.item(fn, load=(w_ap_fn(n0), (nk, 256)))

    def barrier(self):
        S = self.S
        for e in ("pe", "act", "dve", "pool", "sp"):
            for o in ("pe", "act", "dve", "pool"):
                if o != e and S.cnt[o] > 0:
                    S.need(e, (o, S.cnt[o]))
            for q in ("sp", "pool"):
                k = S.dma_k[q]
                for sl in range(min(k, S.NS)):
                    last = k - 1 - ((k - 1 - sl) % S.NS)
                    S.need(e, (("dma", q, sl), 16 * (last // S.NS + 1)))

    def ffn(self, l, tt, mid=None):
        S = self.S
        dr = self.dr
        actT = self.U24
        self.item(lambda wv, wr: self.pre_norm(f"ffn_pre{l}", tt))
        for i0 in range(0, NPAIR, 2):
            def fn(wv, wres, i0=i0):
                pending = []
                for di in range(2):
                    i = i0 + di
                    st_ = (i % 3) * 4
                    for half in range(2):
                        ch = i + half * NPAIR
                        pi = 2 + half + 2 * (i % 2)
                        ps = self.ps[pi]
                        for k in range(8):
                            S.op("pe", lambda e, k=k, di=di, half=half, ps=ps: e.matmul(
                                ps[:], lhsT=wv[:, di, k, half * 128:(half + 1) * 128], rhs=self.hT[:, k, :],
                                start=(k == 0), stop=(k == 7)), r=[wres, "hT"], w=[f"ps{pi}"])
                        hu, hres = self.ftile(st_ + half)
                        acc, ares = self.ftile(st_ + 2 + half)
                        S.op("pool", lambda e, ch=ch, hu=hu: e.tensor_copy(out=hu[:, 0:2], in_=self.fhalo[:, ch, :]),
                             r=[f"fhalo{ch}"], w=[hres + "h"])
                        S.op("act", lambda e, hu=hu, ps=ps: e.activation(out=hu[:, 2:2 + T], in_=ps[:], func=AF.Copy),
                             r=[f"ps{pi}"], w=[hres])
                        S.op("pool", lambda e, ch=ch, hu=hu: e.tensor_copy(out=self.fhalo[:, ch, :], in_=hu[:, T:T + 2]),
                             r=[hres], w=[f"fhalo{ch}"])
                        S.op("act", lambda e, ch=ch, hu=hu, acc=acc: e.activation(
                            out=acc[:, 0:T], in_=hu[:, 0:T], func=AF.Identity,
                            bias=self.pcol(f"fcb{l}", ch), scale=self.pcol(f"fcw{l}_0", ch)),
                            r=[hres, hres + "h", "pv"], w=[ares])
                    for f in pending:
                        f()
                    pending = []
                    for half in range(2):
                        ch = i + half * NPAIR
                        hu, hres = self.ftile(st_ + half)
                        acc, ares = self.ftile(st_ + 2 + half)
                        for kk in (1, 2):
                            S.op("dve", lambda e, ch=ch, hu=hu, acc=acc, kk=kk: e.scalar_tensor_tensor(
                                out=acc[:, 0:T], in0=hu[:, kk:kk + T], scalar=self.pcol(f"fcw{l}_{kk}", ch),
                                in1=acc[:, 0:T], op0=ALU.mult, op1=ALU.add), r=[hres, hres + "h", "pv", ares], w=[ares])
                    ag, agres = self.ftile(st_ + 2)
                    av, avres = self.ftile(st_ + 3)
                    pending.append(lambda ag=ag, agres=agres: S.op(
                        "act", lambda e: e.activation(out=ag[:, 0:T], in_=ag[:, 0:T], func=AF.Gelu_apprx_tanh), r=[agres], w=[agres]))
                    pending.append(lambda i=i, ag=ag, av=av, agres=agres, avres=avres: S.op(
                        "pool", lambda e: e.tensor_tensor(out=actT[:, i, :], in0=ag[:, 0:T], in1=av[:, 0:T], op=ALU.mult),
                        r=[agres, avres], w=["actT"]))
                for f in pending:
                    f()
            self.item(fn, load=(dr[f"ffn_w_up{l}"][:, i0:i0 + 2, :, :], (2, 8, 256)))
        for n in range(8):
            def fn(wv, wres, n=n):
                pi = n % 2
                ps = self.ps[pi]
                for k in range(NPAIR):
                    S.op("pe", lambda e, k=k, ps=ps: e.matmul(ps[:], lhsT=wv[:, k, :], rhs=actT[:, k, :],
                                                              start=(k == 0), stop=(k == NPAIR - 1)),
                         r=[wres, "actT"], w=[f"ps{pi}"])
                S.op("act", lambda e, ps=ps: e.activation(out=self.m_sb[:, n, :], in_=ps[:], func=AF.Copy),
                     r=[f"ps{pi}"], w=["m_sb"])
            self.item(fn, load=(dr[f"ffn_w_down{l}"][:, n, :, :], (NPAIR, 128)))
            if n == 3 and mid is not None:
                mid()
        self.item(lambda wv, wr: self.post_norm_residual(f"ffn_post{l}", tt))

    def even_layer(self, l, j):
        S = self.S
        dr = self.dr
        nc = self.nc
        es2 = ExitStack()
        def sb(name, shape, dt):
            return es2.enter_context(nc.sbuf_tensor(f"{name}_{l}", list(shape), dt))

        def begin(wv, wr):
            self.barrier()
            self._es2 = es2
            self.nkT = sb("nkT", (128, 4, SEQ), BF16)
            self.v_sb = sb("v_sb", (128, 16, 512), BF16)
            self.wsT = sb("wsT", (128, 8, 128), BF16)
            self.qT = sb("qT", (128, 4, T), BF16)
            self.Rb = [sb(f"Rb{i}", (128, T), BF16) for i in range(2)]
            self.w_t = [sb(f"w_t{i}", (128, T), BF16) for i in range(2)]
            self.st8 = sb("st8", (128, 8), F32)
            self.st8b = sb("st8b", (128, 8), F32)
            self.fhalo = sb("fhalo", (128, 44, 2), F32)
            print("[sbuf] even layer remaining bytes/partition:", nc.sbuf_bytes_remaining)
            S.op("dve", lambda e: e.memset(self.fhalo[:], 0.0), w=[f"fhalo{c_}" for c_ in range(44)])
            for i_ in range(2):
                S.op("dve", lambda e, i_=i_: e.memset(self.w_t[i_][:], 0.0), w=[f"w_t{i_}"])
            S.dma("pool", self.wsT[:], dr[f"sgu_wT{j}"][:, :, :], w=["wsT"])
            S.op("dve", lambda e: e.tensor_tensor(
                out=self.wsT[:], in0=self.wsT[:],
                in1=self.cb(C_LE).unsqueeze(1).to_broadcast([128, 8, 128]), op=ALU.mult),
                r=["wsT", "cst"], w=["wsT"])
        self.item(begin)
        for tt in range(NT):
            hoist = self.do_mixer and self.do_ffn and tt + 1 < NT
            if self.do_mixer:
                self.even_mixer_tile(l, j, tt, skip_pre=(hoist_prev if tt > 0 else False))
            if self.do_ffn:
                self.ffn(l, tt, mid=(lambda tt=tt: self.item(lambda wv, wr: self.pre_norm(f"mix_pre{l}", tt + 1))) if hoist else None)
            hoist_prev = hoist
        def end(wv, wr):
            self.barrier()
            es2.close()
            self._es2 = None
        self.item(end)

    def even_mixer_tile(self, l, j, tt, skip_pre=False):
        S = self.S
        dr = self.dr
        ts = slice(tt * T, (tt + 1) * T)
        mixT = self.hT
        sp_all = self.U24
        if not skip_pre:
            self.item(lambda wv, wr: self.pre_norm(f"mix_pre{l}", tt))
        W = dr[f"ab_w_in{j}"]

        def proj_fm(wv, wres, kind):
            for c in range(4):
                pi = c % 4
                ps = self.ps[pi]
                for k in range(8):
                    S.op("pe", lambda e, k=k, c=c, ps=ps: e.matmul(ps[:], lhsT=wv[:, k, c * 128:(c + 1) * 128],
                                                                   rhs=self.hT[:, k, :], start=(k == 0), stop=(k == 7)),
                         r=[wres, "hT"], w=[f"ps{pi}"])
                if kind == "q":
                    S.op("act", lambda e, c=c, ps=ps: e.activation(out=self.qT[:, c, :], in_=ps[:], func=AF.Copy, scale=0.125),
                         r=[f"ps{pi}"], w=["qT"])
                elif kind == "k":
                    S.op("act", lambda e, c=c, ps=ps: e.activation(out=self.nkT[:, c, ts], in_=ps[:], func=AF.Copy, scale=-1.0),
                         r=[f"ps{pi}"], w=["nkT"])
                else:
                    S.op("act", lambda e, c=c, ps=ps: e.activation(out=self.scr_bf(2, c), in_=ps[:], func=AF.Gelu_apprx_tanh),
                         r=[f"ps{pi}"], w=[f"scr{2 + c // 2}"])

        def proj_tm(wv, wres, kind):
            for b in range(4):
                pi = b % 4
                ps = self.ps[pi]
                for k in range(8):
                    S.op("pe", lambda e, k=k, b=b, ps=ps: e.matmul(ps[:], lhsT=self.hT[:, k, b * 128:(b + 1) * 128],
                                                                   rhs=wv[:, k, :], start=(k == 0), stop=(k == 7)),
                         r=[wres, "hT"], w=[f"ps{pi}"])
                if kind == "v":
                    S.op("act", lambda e, b=b, ps=ps: e.activation(out=self.v_sb[:, tt * 4 + b, :], in_=ps[:], func=AF.Copy),
                         r=[f"ps{pi}"], w=["v_sb"])
                else:
                    vg = self.scr(0)
                    vg3 = vg.rearrange("p (g d) -> p g d", g=8)
                    tB = self.scr(1).rearrange("p (g d) -> p g d", g=8)
                    S.op("act", lambda e, ps=ps: e.activation(out=vg, in_=ps[:], func=AF.Gelu_apprx_tanh),
                         r=[f"ps{pi}"], w=["scr0"])
                    S.op("dve", lambda e: e.tensor_reduce(out=self.st8[:], in_=vg3, axis=AX.X, op=ALU.add),
                         r=["scr0"], w=["st8"])
                    S.op("dve", lambda e: e.tensor_scalar(out=self.st8[:], in0=self.st8[:], scalar1=-1.0 / 64, scalar2=None,
                                                          op0=ALU.mult), r=["st8"], w=["st8"])
                    S.op("dve", lambda e: e.tensor_tensor(out=vg3, in0=vg3,
                                                          in1=self.st8[:].unsqueeze(2).to_broadcast([128, 8, 64]), op=ALU.add),
                         r=["scr0", "st8"], w=["scr0"])
                    S.op("dve", lambda e: e.tensor_tensor(out=tB, in0=vg3, in1=vg3, op=ALU.mult),
                         r=["scr0"], w=["scr1"])
                    S.op("dve", lambda e: e.tensor_reduce(out=self.st8b[:], in_=tB, axis=AX.X, op=ALU.add),
                         r=["scr1"], w=["st8b"])
                    S.op("act", lambda e: e.activation(out=self.st8b[:], in_=self.st8b[:], func=AF.Ln,
                                                       bias=self.kc[:, 0:1], scale=1.0 / 64),
                         r=["st8b", "kc"], w=["st8b"])
                    S.op("act", lambda e: e.activation(out=self.st8b[:], in_=self.st8b[:], func=AF.Exp, scale=-0.5),
                         r=["st8b"], w=["st8b"])
                    S.op("dve", lambda e, b=b: e.tensor_tensor(
                        out=self.scr_bf(4, b).rearrange("p (g d) -> p g d", g=8), in0=vg3,
                        in1=self.st8b[:].unsqueeze(2).to_broadcast([128, 8, 64]), op=ALU.mult),
                        r=["scr0", "st8b"], w=[f"scr{4 + b // 2}"])

        kinds = ["q", "k", "v", "u", "vg"]
        for kind in ["vg", "u", "q", "k", "v"]:
            gi = kinds.index(kind)
            f = proj_fm if kind in ("q", "k", "u") else proj_tm
            self.item(lambda wv, wr, kind=kind, f=f: f(wv, wr, kind), load=(W[:, :, gi * 512:(gi + 1) * 512], (8, 512)))

        def sgu(wv, wr):
            for c in range(4):
                pi = c % 2
                ps = self.ps[pi]
                S.dma("sp", self.scr(1), dr[f"sgub{j}"][:, c * 512:(c + 1) * 512], w=["scr1"])
                for jj in range(4):
                    for gg in range(2):
                        g = 2 * c + gg
                        S.op("pe", lambda e, g=g, gg=gg, jj=jj, ps=ps: e.matmul(
                            ps[gg * 64:(gg + 1) * 64, jj * 128:(jj + 1) * 128],
                            lhsT=self.scr_bf(4, jj)[:, g * 64:(g + 1) * 64], rhs=self.wsT[:, g, :], start=True, stop=True),
                            r=["scr4", "scr5", "wsT"], w=[f"ps{pi}"])
                S.op("dve", lambda e, ps=ps: e.tensor_tensor(out=self.scr(0), in0=ps[:], in1=self.scr(1), op=ALU.add),
                     r=[f"ps{pi}", "scr1"], w=["scr0"])
                S.op("dve", lambda e, c=c: e.tensor_tensor(out=mixT[:, 4 + c, :], in0=self.scr(0), in1=self.scr_bf(2, c), op=ALU.mult),
                     r=["scr0", f"scr{2 + c // 2}"], w=["hT"])
        self.item(sgu)

        nkb = 4 * (tt + 1)
        def attn_head(wv, wr, h):
            c = h // 2
            r0 = (h % 2) * 64
            qh = self.qT[r0:r0 + 64, c, :]
            for kb in range(nkb):
                pi = 2 + kb % 2
                ps = self.ps[pi]
                ei = kb % 2
                kl = kb - 4 * tt
                c0 = max(kl, 0) * 128
                S.op("pe", lambda e, kb=kb, ps=ps, c0=c0: e.matmul(ps[:, c0:T], lhsT=self.nkT[r0:r0 + 64, c, kb * 128:(kb + 1) * 128],
                                                                   rhs=qh[:, c0:T], start=True, stop=True),
                     r=["nkT", "qT"], w=[f"ps{pi}"])
                S.op("act", lambda e, ps=ps, ei=ei, c0=c0: e.activation(out=self.scr(ei, c0, T), in_=ps[:, c0:T], func=AF.Exp, scale=-1.0),
                     r=[f"ps{pi}"], w=[f"scr{ei}"])
                S.op("act", lambda e, kb=kb, ei=ei, c0=c0: e.activation(out=sp_all[:, kb, c0:T], in_=self.scr(ei, c0, T), func=AF.Ln,
                                                                        bias=self.kc[:, 1:2]),
                     r=[f"scr{ei}", "kc"], w=[f"sp{kb}"])
                if kl >= 0:
                    S.op("dve", lambda e, kb=kb, kl=kl, c0=c0: e.tensor_tensor(
                        out=sp_all[:, kb, c0:T], in0=sp_all[:, kb, c0:T], in1=self.amask(kl)[:, c0:T], op=ALU.mult),
                        r=[f"sp{kb}", "cst"], w=[f"sp{kb}"])
            opi = 6
            ops = self.ps[opi]
            order = list(range(nkb - 1, -1, -1))

            def emitE(idx):
                kb = order[idx]
                pi = 4 + idx % 2
                ps = self.ps[pi]
                last = (kb == nkb - 1)
                if not last:
                    if kb == nkb - 2:
                        S.op("dve", lambda e: e.tensor_copy(out=self.Rb[kb % 2][:], in_=sp_all[:, kb + 1, :]),
                             r=[f"sp{kb + 1}"], w=[f"Rb{kb % 2}"])
                    else:
                        S.op("dve", lambda e: e.tensor_tensor(out=self.Rb[kb % 2][:], in0=self.Rb[(kb + 1) % 2][:],
                                                              in1=sp_all[:, kb + 1, :], op=ALU.add),
                             r=[f"sp{kb + 1}", f"Rb{(kb + 1) % 2}"], w=[f"Rb{kb % 2}"])
                c0 = max(kb - 4 * tt, 0) * 128
                S.op("pe", lambda e: e.matmul(ps[:, c0:T], lhsT=self.cb(C_GE), rhs=sp_all[:, kb, c0:T], start=True, stop=False),
                     r=[f"sp{kb}", "cst"], w=[f"ps{pi}"])
                if not last:
                    S.op("pe", lambda e: e.matmul(ps[:, c0:T], lhsT=self.cb(C_ONE), rhs=self.Rb[kb % 2][:, c0:T], start=False, stop=False),
                         r=[f"Rb{kb % 2}", "cst"], w=[f"ps{pi}"])
                S.op("pe", lambda e: e.matmul(ps[:, c0:T], lhsT=self.nkT[r0:r0 + 64, c, kb * 128:(kb + 1) * 128], rhs=qh[:, c0:T],
                                              start=False, stop=True), r=["nkT", "qT"], w=[f"ps{pi}"])

            def emitW(idx):
                kb = order[idx]
                pi = 4 + idx % 2
                ps = self.ps[pi]
                wt = self.w_t[idx % 2]
                wres = f"w_t{idx % 2}"
                c0 = max(kb - 4 * tt, 0) * 128
                S.op("act", lambda e: e.activation(out=wt[:, c0:T], in_=ps[:, c0:T], func=AF.Exp, scale=-1.0), r=[f"ps{pi}"], w=[wres])
                kl = kb - 4 * tt
                if kl >= 0:
                    S.op("dve", lambda e: e.tensor_tensor(out=wt[:], in0=wt[:], in1=self.amask(kl), op=ALU.mult),
                         r=[wres, "cst"], w=[wres])

            def emitPV(idx):
                kb = order[idx]
                wt = self.w_t[idx % 2]
                wres = f"w_t{idx % 2}"
                S.op("pe", lambda e: e.matmul(ops[r0:r0 + 64, :], lhsT=self.v_sb[:, kb, h * 64:(h + 1) * 64], rhs=wt[:],
                                              start=(idx == 0), stop=(idx == nkb - 1)), r=["v_sb", wres], w=["ps6"])

            emitE(0)
            for idx in range(nkb):
                if idx + 1 < nkb:
                    emitE(idx + 1)
                emitW(idx)
                emitPV(idx)
            S.op("act", lambda e: e.activation(out=mixT[r0:r0 + 64, c, :], in_=ops[r0:r0 + 64, :], func=AF.Copy),
                 r=["ps6"], w=["hT"])
        def zero_masked(wv, wr):
            for kl in (1, 2, 3):
                kb = 4 * tt + kl
                S.op("dve", lambda e, kb=kb, kl=kl: e.memset(sp_all[:, kb, 0:kl * 128], 0.0), w=[f"sp{kb}"])
        self.item(zero_masked)
        for h in range(8):
            self.item(lambda wv, wr, h=h: attn_head(wv, wr, h))

        Wo = dr[f"ab_w_out{j}"]
        self.out_proj(lambda n0: Wo[:, :, n0 * 128:(n0 + 2) * 128], 8, lambda k: mixT[:, k, :], ["hT"])
        self.item(lambda wv, wr: self.post_norm_residual(f"mix_post{l}", tt))

    def odd_layer(self, l, j):
        S = self.S
        nc = self.nc
        es2 = ExitStack()
        def sb(name, shape, dt):
            return es2.enter_context(nc.sbuf_tensor(f"{name}_{l}", list(shape), dt))

        def begin(wv, wr):
            self.barrier()
            self._es2 = es2
            self.St = sb("St", (128, 8, 256), F32)
            self.Stb = sb("Stb", (128, 8, 256), BF16)
            self.shalo = sb("shalo", (128, 32, 3), F32)
            self.negA = sb("negA", (128, 32), F32)
            self.zsT = sb("zsT", (128, 16, T), BF16)
            self.dt_tok = sb("dt_tok", (128, 4, 32), F32)
            self.a_b = sb("a_b", (128, 4, 32), BF16)
            self.xp = sb("xp", (128, 2048), BF16)
            self.xd = sb("xd", (128, 256), BF16)
            self.Btok = sb("Btok", (128, 1024), BF16)
            self.aTri2 = [sb(f"aTri{i}", (128, 4, 128), BF16) for i in range(2)]
            self.decay = sb("decay", (128, 32), F32)
            self.cd = sb("cd", (128, 32), F32)
            self.cbm = sb("cbm", (128, 128), BF16)
            self.expD2 = [sb(f"expD{i}", (128, 4, 128), BF16) for i in range(2)]
            self.MT2 = [sb(f"MT{i}", (128, 4, 128), BF16) for i in range(2)]
            self.Eac = sb("Eac", (128, 4, 128), BF16)
            self.Cp2 = [sb(f"Cp{i}", (128, 4, 128), BF16) for i in range(2)]
            self.stmp = sb("stmp", (128, 256), F32)
            self.fhalo = sb("fhalo", (128, 44, 2), F32)
            print("[sbuf] odd layer remaining bytes/partition:", nc.sbuf_bytes_remaining)
            S.op("dve", lambda e: e.memset(self.fhalo[:], 0.0), w=[f"fhalo{c_}" for c_ in range(44)])
            S.op("dve", lambda e: e.memset(self.St[:], 0.0), w=[f"St{g_}" for g_ in range(8)])
            S.op("dve", lambda e: e.memset(self.Stb[:], 0.0), w=[f"Stb{g_}" for g_ in range(8)])
            S.op("dve", lambda e: e.memset(self.shalo[:], 0.0), w=[f"shalo{c_}" for c_ in range(32)])
            S.op("act", lambda e: e.activation(out=self.negA[:], in_=self.rrow(f"alog{j}"), func=AF.Exp), r=["rv"], w=["negA"])
            S.op("dve", lambda e: e.tensor_scalar(out=self.negA[:], in0=self.negA[:], scalar1=-1.0, scalar2=None, op0=ALU.mult),
                 r=["negA"], w=["negA"])
        self.item(begin)
        for tt in range(NT):
            hoist = self.do_mixer and self.do_ffn and tt + 1 < NT
            if self.do_mixer:
                self.ssd_tile(l, j, tt, skip_pre=(hoist_prev if tt > 0 else False))
            if self.do_ffn:
                self.ffn(l, tt, mid=(lambda tt=tt: self.item(lambda wv, wr: self.pre_norm(f"mix_pre{l}", tt + 1))) if hoist else None)
            hoist_prev = hoist
        def end(wv, wr):
            self.barrier()
            es2.close()
            self._es2 = None
        self.item(end)

    def ssd_tile(self, l, j, tt, skip_pre=False):
        S = self.S
        dr = self.dr
        xsT = self.U24
        def BT(g, cs):
            return self.U24[:, 16 + g, cs]
        def CT(g, cs):
            return self.MB[:, g, cs]
        if not skip_pre:
            self.item(lambda wv, wr: self.pre_norm(f"mix_pre{l}", tt))
        W = dr[f"ssd_w_in{j}"]

        def zproj(wv, wres, q):
            for cc in range(4):
                c = q * 4 + cc
                pi = cc % 4
                ps = self.ps[pi]
                for k in range(8):
                    S.op("pe", lambda e, k=k, cc=cc, ps=ps: e.matmul(ps[:], lhsT=wv[:, k, cc * 128:(cc + 1) * 128], rhs=self.hT[:, k, :],
                                                                     start=(k == 0), stop=(k == 7)), r=[wres, "hT"], w=[f"ps{pi}"])
                S.op("act", lambda e, c=c, ps=ps: e.activation(out=self.zsT[:, c, :], in_=ps[:], func=AF.Silu),
                     r=[f"ps{pi}"], w=["zsT"])

        def xbcproj(wv, wres, q):
            pending = []
            for cc in range(4):
                ch = q * 4 + cc
                pi = cc % 4
                ps = self.ps[pi]
                for k in range(8):
                    S.op("pe", lambda e, k=k, cc=cc, ps=ps: e.matmul(ps[:], lhsT=wv[:, k, cc * 128:(cc + 1) * 128], rhs=self.hT[:, k, :],
                                                                     start=(k == 0), stop=(k == 7)), r=[wres, "hT"], w=[f"ps{pi}"])
                si = 2 * (ch % 3)
                ai = si + 1
                S.op("pool", lambda e, ch=ch, si=si: e.tensor_copy(out=self.SCR[:, si, 0:3], in_=self.shalo[:, ch, :]), r=[f"shalo{ch}"], w=[f"scr{si}h"])
                S.op("act", lambda e, ps=ps, si=si: e.activation(out=self.SCR[:, si, 3:3 + T], in_=ps[:], func=AF.Copy), r=[f"ps{pi}"], w=[f"scr{si}"])
                S.op("pool", lambda e, ch=ch, si=si: e.tensor_copy(out=self.shalo[:, ch, :], in_=self.SCR[:, si, T:T + 3]), r=[f"scr{si}"], w=[f"shalo{ch}"])
                S.op("act", lambda e, ch=ch, si=si, ai=ai: e.activation(out=self.scr(ai), in_=self.SCR[:, si, 0:T], func=AF.Identity,
                                                                        bias=self.pcol(f"scb{j}", ch), scale=self.pcol(f"scw{j}_0", ch)),
                     r=[f"scr{si}", f"scr{si}h", "pv"], w=[f"scr{ai}"])
                for f in pending:
                    f()
                pending = []
                for kk in (1, 2, 3):
                    S.op("dve", lambda e, ch=ch, kk=kk, si=si, ai=ai: e.scalar_tensor_tensor(
                        out=self.scr(ai), in0=self.SCR[:, si, kk:kk + T], scalar=self.pcol(f"scw{j}_{kk}", ch), in1=self.scr(ai),
                        op0=ALU.mult, op1=ALU.add), r=[f"scr{si}", f"scr{si}h", "pv", f"scr{ai}"], w=[f"scr{ai}"])
                if ch < 16:
                    dst, dres = xsT[:, ch, :], "xsT"
                elif ch < 24:
                    dst, dres = BT(ch - 16, slice(0, T)), "BT"
                else:
                    dst, dres = CT(ch - 24, slice(0, T)), "m_sb"
                pending.append(lambda dst=dst, ai=ai, dres=dres: S.op(
                    "act", lambda e: e.activation(out=dst, in_=self.scr(ai), func=AF.Silu), r=[f"scr{ai}"], w=[dres]))
            for f in pending:
                f()
        for q in range(8):
            if q % 2 == 0:
                qz = q // 2
                self.item(lambda wv, wr, qz=qz: zproj(wv, wr, qz), load=(W[:, :, qz * 512:(qz + 1) * 512], (8, 512)))
            self.item(lambda wv, wr, q=q: xbcproj(wv, wr, q), load=(W[:, :, 2048 + q * 512:2048 + (q + 1) * 512], (8, 512)))

        def dtproj(wv, wres):
            ps = self.ps[0]
            for b in range(4):
                for k in range(8):
                    S.op("pe", lambda e, k=k, b=b: e.matmul(ps[:, b * 32:(b + 1) * 32], lhsT=self.hT[:, k, b * 128:(b + 1) * 128],
                                                            rhs=wv[:, k, :], start=(k == 0), stop=(k == 7)),
                         r=[wres, "hT"], w=["ps0"])
            d3 = self.dt_tok[:]
            S.op("dve", lambda e: e.tensor_tensor(out=d3, in0=ps[:, 0:128].rearrange("p (b h) -> p b h", b=4),
                                                  in1=self.rrow(f"dtb{j}").unsqueeze(1).to_broadcast([128, 4, 32]), op=ALU.add),
                 r=["ps0", "rv"], w=["dt_tok"])
            S.op("act", lambda e: e.activation(out=d3, in_=d3, func=AF.Exp), r=["dt_tok"], w=["dt_tok"])
            S.op("act", lambda e: e.activation(out=d3, in_=d3, func=AF.Ln, bias=self.kc[:, 1:2]), r=["dt_tok", "kc"], w=["dt_tok"])
            S.op("dve", lambda e: e.tensor_tensor(out=self.a_b[:], in0=d3,
                                                  in1=self.negA[:].unsqueeze(1).to_broadcast([128, 4, 32]), op=ALU.mult),
                 r=["dt_tok", "negA"], w=["a_b"])
        self.item(dtproj, load=(W[:, :, 6144:6176], (8, 32)))

        def chunk(wv, wr, jc):
            cs = slice(jc * 128, (jc + 1) * 128)
            ygall = self.MBf[:, 2048:4096].rearrange("p (c l) -> p c l", c=16)
            idb = self.cb(C_ID)
            for bq in range(4):
                for cc in range(4):
                    c = bq * 4 + cc
                    S.op("pe", lambda e, c=c, cc=cc: e.matmul(self.ps[7][:, cc * 128:(cc + 1) * 128], lhsT=xsT[:, c, cs], rhs=idb,
                                                              start=True, stop=True),
                         r=["xsT", "cst"], w=["ps7"])
                S.op("dve", lambda e, bq=bq: e.tensor_tensor(
                    out=self.xp[:, bq * 512:(bq + 1) * 512].rearrange("p (h d) -> p h d", h=8),
                    in0=self.ps[7][:].rearrange("p (h d) -> p h d", h=8),
                    in1=self.dt_tok[:, jc, bq * 8:(bq + 1) * 8].unsqueeze(2).to_broadcast([128, 8, 64]), op=ALU.mult),
                    r=["ps7", "dt_tok"], w=["xp"])
            if getattr(self, "cut", 99) <= 1:
                return
            for half in range(2):
                for gg in range(4):
                    g = half * 4 + gg
                    S.op("pe", lambda e, g=g, gg=gg: e.matmul(self.ps[6][:, gg * 128:(gg + 1) * 128], lhsT=BT(g, cs), rhs=idb,
                                                              start=True, stop=True),
                         r=["BT", "cst"], w=["ps6"])
                S.op("act", lambda e, half=half: e.activation(out=self.Btok[:, half * 512:(half + 1) * 512], in_=self.ps[6][:], func=AF.Copy),
                     r=["ps6"], w=["Btok"])
            if getattr(self, "cut", 99) <= 2:
                return
            ps0 = self.ps[0]
            S.op("pe", lambda e: e.matmul(ps0[:, 0:32], lhsT=self.cb(C_GT), rhs=self.a_b[:, jc, :], start=True, stop=True),
                 r=["cst", "a_b"], w=["ps0"])
            S.op("pe", lambda e: e.matmul(ps0[:, 32:64], lhsT=self.cb(C_ONE), rhs=self.a_b[:, jc, :], start=True, stop=True),
                 r=["cst", "a_b"], w=["ps0"])
            S.op("act", lambda e: e.activation(out=self.decay[:], in_=ps0[:, 0:32], func=AF.Exp), r=["ps0"], w=["decay"])
            S.op("act", lambda e: e.activation(out=self.cd[:], in_=ps0[:, 32:64], func=AF.Exp), r=["ps0"], w=["cd"])
            if getattr(self, "cut", 99) <= 3:
                return
            def head(g):
                h0 = 4 * g
                p = g % 2
                aTri, expD, MT, Cp = self.aTri2[p], self.expD2[p], self.MT2[p], self.Cp2[p]
                bD, bA = 2, 3
                bY, yo, yres = (4, 0, 'ps4') if p == 0 else (0, 0, 'ps0')
                bS = 5 + p
                S.op("pool", lambda e: e.tensor_tensor(
                    out=aTri[:], in0=self.cb(C_LE).unsqueeze(1).to_broadcast([128, 4, 128]),
                    in1=self.a_b[:, jc, h0:h0 + 4].unsqueeze(2).to_broadcast([128, 4, 128]), op=ALU.mult),
                    r=["cst", "a_b"], w=[f"aTri{p}"])
                S.op("pool", lambda e: e.tensor_tensor(
                    out=self.xd[:].rearrange("p (h d) -> p h d", h=4),
                    in0=self.xp[:, g * 256:(g + 1) * 256].rearrange("p (h d) -> p h d", h=4),
                    in1=self.decay[:, h0:h0 + 4].unsqueeze(2).to_broadcast([128, 4, 64]), op=ALU.mult),
                    r=["xp", "decay"], w=["xd"])
                aT2 = aTri[:].rearrange("p h l -> p (h l)")
                ps1 = self.ps[1]
                S.op("pe", lambda e: e.matmul(ps1[:, 0:128], lhsT=BT(g, cs), rhs=CT(g, cs), start=True, stop=True),
                     r=["BT", "m_sb"], w=["ps1"])
                S.op("dve", lambda e: e.tensor_tensor(out=self.cbm[:], in0=ps1[:, 0:128], in1=self.cb(C_LE), op=ALU.mult),
                     r=["ps1", "cst"], w=["cbm"])
                psD = self.ps[bD]
                S.op("pe", lambda e: e.matmul(psD[:], lhsT=self.cb(C_GT), rhs=aT2, start=True, stop=True), r=["cst", f"aTri{p}"], w=[f"ps{bD}"])
                S.op("act", lambda e: e.activation(out=expD[:].rearrange("p h l -> p (h l)"), in_=psD[:], func=AF.Exp),
                     r=[f"ps{bD}"], w=[f"expD{p}"])
                S.op("dve", lambda e: e.tensor_tensor(out=MT[:], in0=expD[:],
                                                      in1=self.cbm[:].unsqueeze(1).to_broadcast([128, 4, 128]), op=ALU.mult),
                     r=[f"expD{p}", "cbm"], w=[f"MT{p}"])
                psA = self.ps[bA]
                S.op("pe", lambda e: e.matmul(psA[:], lhsT=self.cb(C_ONE), rhs=aT2, start=True, stop=True), r=["cst", f"aTri{p}"], w=[f"ps{bA}"])
                S.op("act", lambda e: e.activation(out=self.Eac[:].rearrange("p h l -> p (h l)"), in_=psA[:], func=AF.Exp),
                     r=[f"ps{bA}"], w=["Eac"])
                S.op("pool", lambda e: e.tensor_tensor(out=Cp[:], in0=self.Eac[:],
                                                       in1=CT(g, cs).unsqueeze(1).to_broadcast([128, 4, 128]), op=ALU.mult),
                     r=["Eac", "m_sb"], w=[f"Cp{p}"])
                psY = self.ps[bY]
                for r_ in range(4):
                    h = h0 + r_
                    cl = r_ // 2
                    ro = (h % 2) * 64
                    S.op("pe", lambda e, h=h, r_=r_, cl=cl, ro=ro: e.matmul(
                        psY[ro:ro + 64, yo + cl * 128:yo + (cl + 1) * 128], lhsT=self.xp[:, h * 64:(h + 1) * 64], rhs=MT[:, r_, :],
                        start=True, stop=False), r=["xp", f"MT{p}"], w=[yres])
                    S.op("pe", lambda e, r_=r_, cl=cl, ro=ro: e.matmul(
                        psY[ro:ro + 64, yo + cl * 128:yo + (cl + 1) * 128], lhsT=self.Stb[:, g, r_ * 64:(r_ + 1) * 64], rhs=Cp[:, r_, :],
                        start=False, stop=True), r=[f"Stb{g}", f"Cp{p}"], w=[yres])
                ps5 = self.ps[bS]
                S.op("pe", lambda e: e.matmul(ps5[:, 0:256], lhsT=self.Btok[:, g * 128:(g + 1) * 128], rhs=self.xd[:],
                                              start=True, stop=True), r=["Btok", "xd"], w=[f"ps{bS}"])

            def tail(g):
                h0 = 4 * g
                p = g % 2
                bY, yo, yres = (4, 0, 'ps4') if p == 0 else (0, 0, 'ps0')
                psY = self.ps[bY]
                bS = 5 + p
                ps5 = self.ps[bS]
                S.op("dve", lambda e: e.tensor_tensor(
                    out=self.stmp[:].rearrange("p (r d) -> p r d", r=4), in0=self.St[:, g, :].rearrange("p (r d) -> p r d", r=4),
                    in1=self.cd[:, h0:h0 + 4].unsqueeze(2).to_broadcast([128, 4, 64]), op=ALU.mult),
                    r=[f"St{g}", "cd"], w=["stmp"])
                S.op("dve", lambda e: e.tensor_tensor(out=self.St[:, g, :], in0=self.stmp[:], in1=ps5[:, 0:256], op=ALU.add),
                     r=["stmp", f"ps{bS}"], w=[f"St{g}"])
                S.op("act", lambda e: e.activation(out=self.Stb[:, g, :], in_=self.St[:, g, :], func=AF.Copy), r=[f"St{g}"], w=[f"Stb{g}"])
                for cl in range(2):
                    c = 2 * g + cl
                    S.op("dve", lambda e, c=c, cl=cl: e.scalar_tensor_tensor(
                        out=ygall[:, c, :], in0=xsT[:, c, cs], scalar=self.pcol(f"sd{j}", c), in1=psY[:, yo + cl * 128:yo + (cl + 1) * 128],
                        op0=ALU.mult, op1=ALU.add), r=["xsT", "pv", yres], w=["ygall"])
                    S.op("dve", lambda e, c=c, cl=cl: e.tensor_tensor(out=ygall[:, c, :], in0=ygall[:, c, :], in1=self.zsT[:, c, cs], op=ALU.mult),
                         r=["ygall", "zsT"], w=["ygall"])

            head(0)
            for g in range(1, 8):
                head(g)
                tail(g - 1)
            tail(7)
            for q in range(4):
                b = q % 2
                S.op("act", lambda e, q=q, b=b: e.activation(out=self.sq[:, b, :], in_=ygall[:, 4 * q:4 * q + 4, :].rearrange("p c l -> p (c l)"),
                                                             func=AF.Square), r=["ygall"], w=[f"sq{b}"])
                for cc in range(4):
                    c = 4 * q + cc
                    g = c // 2
                    pi = 6 + g // 4
                    S.op("pe", lambda e, cc=cc, b=b, g=g, pi=pi, c=c: e.matmul(
                        self.ps[pi][:, (g % 4) * 128:(g % 4 + 1) * 128], lhsT=self.cb(C_ONE), rhs=self.sq[:, b, cc * 128:(cc + 1) * 128],
                        start=(c % 2 == 0), stop=(c % 2 == 1)), r=[f"sq{b}", "cst"], w=[f"ps{pi}"])
            for hf in range(2):
                S.op("act", lambda e, hf=hf: e.activation(out=self.scr(4 + hf), in_=self.ps[6 + hf][:], func=AF.Ln,
                                                          bias=self.kc[:, 0:1], scale=1.0 / 256), r=[f"ps{6 + hf}", "kc"], w=[f"scr{4 + hf}"])
                S.op("act", lambda e, hf=hf: e.activation(out=self.scr(4 + hf), in_=self.scr(4 + hf), func=AF.Exp, scale=-0.5),
                     r=[f"scr{4 + hf}"], w=[f"scr{4 + hf}"])
                S.op("dve", lambda e, hf=hf: e.tensor_tensor(
                    out=ygall[:, 8 * hf:8 * hf + 8, :].rearrange("p (g two) l -> p g two l", two=2),
                    in0=ygall[:, 8 * hf:8 * hf + 8, :].rearrange("p (g two) l -> p g two l", two=2),
                    in1=self.scr(4 + hf).rearrange("p (g l) -> p g l", g=4).unsqueeze(2).to_broadcast([128, 4, 2, 128]), op=ALU.mult),
                    r=["ygall", f"scr{4 + hf}"], w=["ygall"])
            o_ng, _ = PCOLS[f"sng{j}"]
            S.op("dve", lambda e: e.tensor_tensor(out=self.zsT[:, :, cs], in0=ygall[:],
                                                  in1=self.pv[:, o_ng:o_ng + 16].unsqueeze(2).to_broadcast([128, 16, 128]), op=ALU.mult),
                 r=["ygall", "pv"], w=["zsT"])
        for jc in range(4):
            self.item(lambda wv, wr, jc=jc: chunk(wv, wr, jc))
        Wo = dr[f"ssd_w_out{j}"]
        self.out_proj(lambda n0: Wo[:, :, n0 * 128:(n0 + 2) * 128], 16, lambda k: self.zsT[:, k, :], ["zsT"])
        self.item(lambda wv, wr: self.post_norm_residual(f"mix_post{l}", tt))


def _pk(v, w):
    return np.ascontiguousarray(np.asarray(v, np.float32).reshape(w, 128).T)


def _rows(w, nk):
    w = np.asarray(w, np.float32)
    return np.ascontiguousarray(w.reshape(nk, 128, w.shape[1]).transpose(1, 0, 2))


def prepare_shared(inp, n_layers=DEPTH):
    pv = np.zeros((128, NPV), np.float32)
    def put(name, arr):
        o, w = PCOLS[name]
        assert arr.shape == (128, w), (name, arr.shape)
        pv[:, o:o + w] = arr
    for l in range(DEPTH):
        put(f"mix_pre{l}", _pk(inp["mix_pre_g"][l], 8))
        put(f"mix_post{l}", _pk(inp["mix_post_g"][l], 8))
        put(f"ffn_pre{l}", _pk(inp["ffn_pre_g"][l], 8))
        put(f"ffn_post{l}", _pk(inp["ffn_post_g"][l], 8))
        for k in range(3):
            put(f"fcw{l}_{k}", _pk(inp["ffn_conv_w"][l][k], 44))
        put(f"fcb{l}", _pk(inp["ffn_conv_b"][l], 44))
    for j in range(2):
        for k in range(4):
            put(f"scw{j}_{k}", _pk(inp["ssd_conv_w"][j][k], 32))
        put(f"scb{j}", _pk(inp["ssd_conv_b"][j], 32))
        put(f"sng{j}", _pk(inp["ssd_norm_g"][j], 16))
        put(f"sd{j}", _pk(np.repeat(np.asarray(inp["ssd_d"][j], np.float32), 64), 16))
    rv = np.zeros((128, NRV), np.float32)
    sgub = {}
    for j in range(2):
        b = np.asarray(inp["sgu_b"][j], np.float32)
        blk = np.zeros((128, 4, 4, 128), np.float32)
        for c in range(4):
            blk[0:64, c, :, :] = b[2 * c][None, None, :]
            blk[64:128, c, :, :] = b[2 * c + 1][None, None, :]
        sgub[j] = blk.reshape(128, 2048)
        o, w = RCOLS[f"dtb{j}"]
        rv[:, o:o + w] = np.asarray(inp["ssd_dt_bias"][j], np.float32)[None, :]
        o, w = RCOLS[f"alog{j}"]
        rv[:, o:o + w] = np.asarray(inp["ssd_a_log"][j], np.float32)[None, :]
    m = {"pvec": pv, "rvec": rv, "cst": _consts()}
    for j in range(2):
        m[f"ab_w_in{j}"] = _rows(inp["ab_w_in"][j], 8)
        m[f"ab_w_out{j}"] = _rows(inp["ab_w_out"][j], 8)
        m[f"sgub{j}"] = sgub[j]
        m[f"sgu_wT{j}"] = np.ascontiguousarray(np.asarray(inp["sgu_w"][j], np.float32).transpose(2, 0, 1))
        m[f"ssd_w_in{j}"] = _rows(inp["ssd_w_in"][j], 8)
        m[f"ssd_w_out{j}"] = _rows(inp["ssd_w_out"][j], 16)
    for l in range(DEPTH):
        wu = _rows(inp["ffn_w_up"][l], 8)
        g = wu[:, :, :DFF].reshape(128, 8, NPAIR, 128)
        v = wu[:, :, DFF:].reshape(128, 8, NPAIR, 128)
        m[f"ffn_w_up{l}"] = np.ascontiguousarray(np.concatenate([g, v], axis=3).transpose(0, 2, 1, 3))
        wd = _rows(inp["ffn_w_down"][l], NPAIR)
        m[f"ffn_w_down{l}"] = np.ascontiguousarray(wd.reshape(128, NPAIR, 8, 128).transpose(0, 2, 1, 3))
    return m


def x_to_core(xb):
    return np.ascontiguousarray(np.asarray(xb, np.float32).T.reshape(8, 128, SEQ).transpose(1, 0, 2).reshape(128, 8 * SEQ))


def core_to_x(y):
    return np.ascontiguousarray(y.reshape(128, 8, SEQ).transpose(1, 0, 2).reshape(D, SEQ).T)


_NC_CACHE = {}


def run(inputs, n_layers=DEPTH, do_mixer=True, do_ffn=True, ncores=NB, trace=False):
    x = np.asarray(inputs["x"], np.float32)
    shared = prepare_shared(inputs)
    key = (n_layers, do_mixer, do_ffn)
    if key not in _NC_CACHE:
        _NC_CACHE[key] = Builder(n_layers, do_mixer, do_ffn).build()
    nc = _NC_CACHE[key]
    in_maps = []
    for b in range(ncores):
        mm = dict(shared)
        mm["xT"] = x_to_core(x[b])
        in_maps.append(mm)
    res = run_bass_kernel_spmd(nc, in_maps, core_ids=list(range(ncores)), **({"trace": True} if trace else {}))
    out = np.stack([core_to_x(np.asarray(r["yT"])) for r in res.results], axis=0)
    if trace:
        print("exec_time_ns", res.exec_time_ns)
    return out.astype(np.float32)


def kernel(**inputs):
    return run(inputs)
```
